# Optimizing a Trainium2 kernel written in Bass

```python
import math
import jax
import jax.numpy as jnp
from jax import lax
import numpy as np

D_MODEL = 1024
BATCH = 4
SEQ = 4096
DEPTH = 4

GRID_W = 64
CTX_LEN = 256
DA_HEADS = 4
DA_HEAD_DIM = 64
GQ_HEADS = 8
GQ_KV_HEADS = 2
GQ_HEAD_DIM = 64
DN_HEADS = 4
DN_HEAD_DIM = 128
DN_CHUNK = 64
DN_CONV_W = 3
N_BRANCH = 3
D_FF = 2816
FFN_CONV_W = 3
Q_BLOCK = 128
ROPE_DIM = 64
ROPE_THETA = 10000.0
NORM_EPS = 1e-6

DA_QK_COLS = DA_HEADS * 2 * DA_HEAD_DIM
DA_V_COLS = DA_HEADS * 2 * DA_HEAD_DIM
GQ_Q_COLS = GQ_HEADS * GQ_HEAD_DIM
GQ_KV_COLS = GQ_KV_HEADS * GQ_HEAD_DIM
DN_WIDTH = DN_HEADS * DN_HEAD_DIM
DN_DIR_COLS = 2 * DN_HEADS
GATE_COLS = N_BRANCH * D_MODEL
IN_SPLITS = (DA_QK_COLS, DA_QK_COLS, DA_V_COLS, GQ_Q_COLS, GQ_KV_COLS, GQ_KV_COLS, 3 * DN_WIDTH, DN_DIR_COLS, DN_DIR_COLS, DN_WIDTH, GATE_COLS)
N_IN = sum(IN_SPLITS)

kernel_name = 'hybrid_diffattn_gqa_deltanet_prefix_trunk'


def rms_norm(x, g):
    xf = x.astype(jnp.float32)
    y = xf * lax.rsqrt(jnp.mean(xf * xf, axis=-1, keepdims=True) + NORM_EPS)
    return (y * g.astype(jnp.float32)).astype(x.dtype)


def l2_normalize(x):
    xf = x.astype(jnp.float32)
    return xf * lax.rsqrt(jnp.sum(xf * xf, axis=-1, keepdims=True) + NORM_EPS)


def modulate(x, g, shift, scale):
    return rms_norm(x, g) * (1.0 + scale) + shift


def split_in_proj(y):
    offsets = np.cumsum(np.array(IN_SPLITS))[:-1].tolist()
    return jnp.split(y, offsets, axis=-1)


def axial_rope_tables(rows):
    row = jnp.repeat(jnp.arange(rows, dtype=jnp.int32), GRID_W).astype(jnp.float32)
    col = jnp.tile(jnp.arange(GRID_W, dtype=jnp.int32), rows).astype(jnp.float32)
    d_axis = ROPE_DIM // 2
    inv_freq = ROPE_THETA ** (-jnp.arange(0, d_axis, 2, dtype=jnp.float32) / d_axis)
    ang_r = row[:, None] * inv_freq[None, :]
    ang_c = col[:, None] * inv_freq[None, :]
    return (jnp.cos(ang_r), jnp.sin(ang_r), jnp.cos(ang_c), jnp.sin(ang_c))


def _rotate(x, cos, sin):
    x1, x2 = jnp.split(x, 2, axis=-1)
    return jnp.concatenate([x1 * cos - x2 * sin, x1 * sin + x2 * cos], axis=-1)


def apply_axial_rope(x, tables):
    shape = (x.shape[1],) + (1,) * (x.ndim - 3) + (-1,)
    cr, sr, cc, sc = (t.reshape(shape) for t in tables)
    xr, xcol = jnp.split(x.astype(jnp.float32), 2, axis=-1)
    return jnp.concatenate([_rotate(xr, cr, sr), _rotate(xcol, cc, sc)], axis=-1).astype(x.dtype)


def dwconv_centred(x, w):
    width = w.shape[0]
    pad = width // 2
    t = x.shape[1]
    xp = jnp.pad(x, ((0, 0), (pad, pad), (0, 0)))
    y = xp[:, 0:t] * w[0]
    for j in range(1, width):
        y = y + xp[:, j:j + t] * w[j]
    return y


def sweep_query_blocks(attend, queries):
    b, t = queries[0].shape[:2]
    nb = t // Q_BLOCK

    def to_blocks(a):
        return jnp.moveaxis(a.reshape((b, nb, Q_BLOCK) + a.shape[2:]), 1, 0)

    out = lax.map(lambda blk: attend(*blk), tuple(to_blocks(a) for a in queries))
    out = jnp.moveaxis(out, 0, 1)
    return out.reshape((b, t) + out.shape[3:])


def diff_heads(q, k, v, qn_g, kn_g, rope):
    b, t = q.shape[:2]
    q = rms_norm(q.reshape(b, t, DA_HEADS, 2, DA_HEAD_DIM), qn_g)
    k = rms_norm(k.reshape(b, t, DA_HEADS, 2, DA_HEAD_DIM), kn_g)
    if rope is not None:
        q = apply_axial_rope(q, rope)
        k = apply_axial_rope(k, rope)
    v = v.reshape(b, t, DA_HEADS, 2 * DA_HEAD_DIM)
    return (q[..., 0, :], q[..., 1, :], k[..., 0, :], k[..., 1, :], v)


def diff_lambda(lp, lam_init):
    lp = lp.astype(jnp.float32)
    return jnp.exp(jnp.sum(lp[0] * lp[1])) - jnp.exp(jnp.sum(lp[2] * lp[3])) + lam_init


def diff_attend(q1, q2, k1, k2, v, lam):
    scale = DA_HEAD_DIM ** -0.5
    p1 = jax.nn.softmax(jnp.einsum('bqhd,bshd->bhqs', q1, k1).astype(jnp.float32) * scale, axis=-1)
    p2 = jax.nn.softmax(jnp.einsum('bqhd,bshd->bhqs', q2, k2).astype(jnp.float32) * scale, axis=-1)
    return jnp.einsum('bhqs,bshe->bqhe', (p1 - lam * p2).astype(v.dtype), v)


def diff_out(o, subln_g, lam_init):
    b, t = o.shape[:2]
    return (rms_norm(o, subln_g) * (1.0 - lam_init)).reshape(b, t, DA_V_COLS)


def gqa_heads(q, k, v, qn_g, kn_g, rope):
    b, t = q.shape[:2]
    q = rms_norm(q.reshape(b, t, GQ_HEADS, GQ_HEAD_DIM), qn_g)
    k = rms_norm(k.reshape(b, t, GQ_KV_HEADS, GQ_HEAD_DIM), kn_g)
    if rope is not None:
        q = apply_axial_rope(q, rope)
        k = apply_axial_rope(k, rope)
    v = v.reshape(b, t, GQ_KV_HEADS, GQ_HEAD_DIM)
    return (q, k, v)


def gqa_attend(q, k, v):
    b, nq = q.shape[:2]
    group = GQ_HEADS // GQ_KV_HEADS
    qg = q.reshape(b, nq, GQ_KV_HEADS, group, GQ_HEAD_DIM)
    s = jnp.einsum('bqhgd,bshd->bhgqs', qg, k).astype(jnp.float32) * (GQ_HEAD_DIM ** -0.5)
    p = jax.nn.softmax(s, axis=-1).astype(v.dtype)
    o = jnp.einsum('bhgqs,bshd->bqhgd', p, v)
    return o.reshape(b, nq, GQ_HEADS, GQ_HEAD_DIM)


def deltanet_inputs(qkv, b_raw, a_raw, conv_w, a_log, dt_bias):
    b, t = qkv.shape[:2]
    qkv = jax.nn.silu(dwconv_centred(qkv, conv_w))
    q, k, v = jnp.split(qkv, 3, axis=-1)
    q = l2_normalize(q.reshape(b, t, DN_HEADS, DN_HEAD_DIM))
    k = l2_normalize(k.reshape(b, t, DN_HEADS, DN_HEAD_DIM))
    v = v.reshape(b, t, DN_HEADS, DN_HEAD_DIM)
    beta = jax.nn.sigmoid(b_raw.astype(jnp.float32)).reshape(b, t, 2, DN_HEADS)
    g = -jnp.exp(a_log.astype(jnp.float32)) * jax.nn.softplus(
        a_raw.astype(jnp.float32).reshape(b, t, 2, DN_HEADS) + dt_bias.astype(jnp.float32))
    return (q, k, v, g, beta)


def gated_delta_chunked(q, k, v, g, beta, s0, with_output):
    dtype = v.dtype
    b, t, h, dk = k.shape
    dv = v.shape[-1]
    n = t // DN_CHUNK

    def to_chunks(a):
        a = jnp.moveaxis(a.astype(jnp.float32), 2, 1)
        return a.reshape((b, h, n, DN_CHUNK) + a.shape[3:])

    kc, vc, gch, bc = to_chunks(k), to_chunks(v), to_chunks(g), to_chunks(beta)
    gcum = jnp.cumsum(gch, axis=-1)
    idx = jnp.arange(DN_CHUNK)
    incl = idx[:, None] >= idx[None, :]
    strict = idx[:, None] > idx[None, :]
    decay = jnp.where(incl, jnp.exp(jnp.where(incl, gcum[..., :, None] - gcum[..., None, :], 0.0)), 0.0)
    kb = kc * bc[..., None]
    a_low = jnp.where(strict, jnp.einsum('bhncd,bhnsd->bhncs', kb, kc) * decay, 0.0)
    eye = jnp.eye(DN_CHUNK, dtype=jnp.float32)
    tmat = lax.linalg.triangular_solve(a_low + eye, jnp.broadcast_to(eye, a_low.shape),
                                       left_side=True, lower=True, unit_diagonal=True)
    u = tmat @ (vc * bc[..., None])
    w = tmat @ (kb * jnp.exp(gcum)[..., None])
    g_last = gcum[..., -1]
    k_tail = kc * jnp.exp(g_last[..., None] - gcum)[..., None]
    xs = [u, w, k_tail, jnp.exp(g_last)]
    if with_output:
        qc = to_chunks(q) * (dk ** -0.5)
        xs += [jnp.einsum('bhncd,bhnsd->bhncs', qc, kc) * decay, qc * jnp.exp(gcum)[..., None]]
    xs = tuple(jnp.moveaxis(a_, 2, 0) for a_ in xs)

    def step(s, inp):
        u_i, w_i, kt_i, gl_i = inp[:4]
        v_new = u_i - jnp.einsum('bhcd,bhde->bhce', w_i, s)
        s_next = s * gl_i[..., None, None] + jnp.einsum('bhcd,bhce->bhde', kt_i, v_new)
        if with_output:
            qk_i, qd_i = inp[4], inp[5]
            o_i = jnp.einsum('bhcd,bhde->bhce', qd_i, s) + jnp.einsum('bhcs,bhse->bhce', qk_i, v_new)
            return s_next, o_i
        return s_next, None

    s_fin, o = lax.scan(step, s0.astype(jnp.float32), xs)
    if not with_output:
        return None, s_fin
    o = jnp.transpose(o, (1, 0, 3, 2, 4)).reshape(b, t, h, dv)
    return o.astype(dtype), s_fin


def bidir_delta(q, k, v, g, beta, s0_fwd, s0_bwd, with_output):
    flip = lambda a: jnp.flip(a, axis=1)
    o_f, s_f = gated_delta_chunked(q, k, v, g[:, :, 0], beta[:, :, 0], s0_fwd, with_output)
    o_b, s_b = gated_delta_chunked(flip(q), flip(k), flip(v), flip(g[:, :, 1]), flip(beta[:, :, 1]), s0_bwd, with_output)
    o = o_f + flip(o_b) if with_output else None
    return o, s_f, s_b


def deltanet_out(o, z, g):
    b, t = z.shape[:2]
    z = z.reshape(b, t, DN_HEADS, DN_HEAD_DIM)
    return (rms_norm(o, g) * jax.nn.silu(z)).reshape(b, t, DN_WIDTH)


def merge_branches(gate_cols, o_a, o_b, o_c, w_a, w_b, w_c, w_out):
    g_a, g_b, g_c = jnp.split(jax.nn.sigmoid(gate_cols), 3, axis=-1)
    return (g_a * (o_a @ w_a) + g_b * (o_b @ w_b) + g_c * (o_c @ w_c)) @ w_out


def conv_glu(h, w1, conv_w, conv_b, w2):
    a, u = jnp.split(h @ w1, 2, axis=-1)
    a = dwconv_centred(a, conv_w) + conv_b
    return (jax.nn.silu(a) * u) @ w2


def setup_inputs(seed: int = 0) -> dict:
    key = jax.random.key(seed)
    ks = jax.random.split(key, 32)
    nrm = lambda k, shape, s: jax.random.normal(k, shape, jnp.float32) * s
    gain = lambda k, shape: 1.0 + 0.02 * jax.random.normal(k, shape, jnp.float32)
    dt = jnp.exp(jax.random.uniform(ks[15], (DEPTH, 2, DN_HEADS), jnp.float32, math.log(1e-3), math.log(1e-1)))
    return {
        'x': nrm(ks[0], (BATCH, SEQ, D_MODEL), 1.0),
        'c': nrm(ks[1], (BATCH, D_MODEL), 1.0),
        'ctx': nrm(ks[2], (BATCH, CTX_LEN, D_MODEL), 1.0),
        'c_ctx': nrm(ks[3], (D_MODEL,), 1.0),
        'w_mod': nrm(ks[4], (DEPTH, D_MODEL, 6 * D_MODEL), 0.5 * D_MODEL ** -0.5),
        'b_mod': nrm(ks[5], (DEPTH, 6 * D_MODEL), 0.01),
        'norm1_g': gain(ks[6], (DEPTH, D_MODEL)),
        'w_in': nrm(ks[7], (DEPTH, D_MODEL, N_IN), D_MODEL ** -0.5),
        'da_qn_g': gain(ks[8], (DEPTH, DA_HEAD_DIM)),
        'da_kn_g': gain(ks[9], (DEPTH, DA_HEAD_DIM)),
        'da_lambda': nrm(ks[10], (DEPTH, 4, DA_HEAD_DIM), 0.1),
        'da_subln_g': gain(ks[11], (DEPTH, 2 * DA_HEAD_DIM)),
        'gq_qn_g': gain(ks[12], (DEPTH, GQ_HEAD_DIM)),
        'gq_kn_g': gain(ks[13], (DEPTH, GQ_HEAD_DIM)),
        'dn_conv_w': nrm(ks[14], (DEPTH, DN_CONV_W, 3 * DN_WIDTH), DN_CONV_W ** -0.5),
        'dn_a_log': jnp.log(jax.random.uniform(ks[16], (DEPTH, 2, DN_HEADS), jnp.float32, 1.0, 16.0)),
        'dn_dt_bias': dt + jnp.log(-jnp.expm1(-dt)),
        'dn_norm_g': gain(ks[17], (DEPTH, DN_HEAD_DIM)),
        'w_br_a': nrm(ks[18], (DEPTH, DA_V_COLS, D_MODEL), DA_V_COLS ** -0.5),
        'w_br_b': nrm(ks[19], (DEPTH, GQ_Q_COLS, D_MODEL), GQ_Q_COLS ** -0.5),
        'w_br_c': nrm(ks[20], (DEPTH, DN_WIDTH, D_MODEL), DN_WIDTH ** -0.5),
        'w_o': nrm(ks[21], (DEPTH, D_MODEL, D_MODEL), D_MODEL ** -0.5),
        'norm2_g': gain(ks[22], (DEPTH, D_MODEL)),
        'ffn_w1': nrm(ks[23], (DEPTH, D_MODEL, 2 * D_FF), D_MODEL ** -0.5),
        'ffn_conv_w': nrm(ks[24], (DEPTH, FFN_CONV_W, D_FF), FFN_CONV_W ** -0.5),
        'ffn_conv_b': nrm(ks[25], (DEPTH, D_FF), 0.01),
        'ffn_w2': nrm(ks[26], (DEPTH, D_FF, D_MODEL), D_FF ** -0.5),
    }


def reference(x, c, ctx, c_ctx, w_mod, b_mod, norm1_g, w_in, da_qn_g, da_kn_g, da_lambda, da_subln_g,
              gq_qn_g, gq_kn_g, dn_conv_w, dn_a_log, dn_dt_bias, dn_norm_g, w_br_a, w_br_b, w_br_c, w_o,
              norm2_g, ffn_w1, ffn_conv_w, ffn_conv_b, ffn_w2):
    b, n_lat = x.shape[0], x.shape[1]
    n_ctx = ctx.shape[1]
    rows = n_lat // GRID_W
    rope = axial_rope_tables(rows)
    cond_lat = jax.nn.silu(c)[:, None, :]
    cond_ctx = jax.nn.silu(c_ctx)[None, None, :]
    s_zero = jnp.zeros((b, DN_HEADS, DN_HEAD_DIM, DN_HEAD_DIM), jnp.float32)
    xl, xc = x, ctx
    for l in range(DEPTH):
        last = l == DEPTH - 1
        lam_init = 0.8 - 0.6 * math.exp(-0.3 * l)
        mod_l = jnp.split(cond_lat @ w_mod[l] + b_mod[l], 6, axis=-1)
        mod_c = jnp.split(cond_ctx @ w_mod[l] + b_mod[l], 6, axis=-1)

        hl = modulate(xl, norm1_g[l], mod_l[0], mod_l[1])
        hc = modulate(xc, norm1_g[l], mod_c[0], mod_c[1])
        pl = split_in_proj(hl @ w_in[l])
        pc = split_in_proj(hc @ w_in[l])

        lam = diff_lambda(da_lambda[l], lam_init)
        al = diff_heads(pl[0], pl[1], pl[2], da_qn_g[l], da_kn_g[l], rope)
        ac = diff_heads(pc[0], pc[1], pc[2], da_qn_g[l], da_kn_g[l], None)
        k1_all = jnp.concatenate([ac[2], al[2]], axis=1)
        k2_all = jnp.concatenate([ac[3], al[3]], axis=1)
        va_all = jnp.concatenate([ac[4], al[4]], axis=1)
        oa_l = sweep_query_blocks(lambda q1, q2: diff_attend(q1, q2, k1_all, k2_all, va_all, lam), (al[0], al[1]))
        oa_l = diff_out(oa_l, da_subln_g[l], lam_init)

        bl = gqa_heads(pl[3], pl[4], pl[5], gq_qn_g[l], gq_kn_g[l], rope)
        bc = gqa_heads(pc[3], pc[4], pc[5], gq_qn_g[l], gq_kn_g[l], None)
        kb_all = jnp.concatenate([bc[1], bl[1]], axis=1)
        vb_all = jnp.concatenate([bc[2], bl[2]], axis=1)
        ob_l = sweep_query_blocks(lambda q: gqa_attend(q, kb_all, vb_all), (bl[0],))
        ob_l = ob_l.reshape(b, n_lat, GQ_Q_COLS)

        dl = deltanet_inputs(pl[6], pl[7], pl[8], dn_conv_w[l], dn_a_log[l], dn_dt_bias[l])
        dc = deltanet_inputs(pc[6], pc[7], pc[8], dn_conv_w[l], dn_a_log[l], dn_dt_bias[l])
        oc_ctx, s_fwd, s_bwd = bidir_delta(dc[0], dc[1], dc[2], dc[3], dc[4], s_zero, s_zero, not last)
        oc_l, _, _ = bidir_delta(dl[0], dl[1], dl[2], dl[3], dl[4], s_fwd, s_bwd, True)
        oc_l = deltanet_out(oc_l, pl[9], dn_norm_g[l])

        xl = xl + mod_l[2] * merge_branches(pl[10], oa_l, ob_l, oc_l, w_br_a[l], w_br_b[l], w_br_c[l], w_o[l])
        if not last:
            oa_c = diff_out(diff_attend(ac[0], ac[1], ac[2], ac[3], ac[4], lam), da_subln_g[l], lam_init)
            ob_c = gqa_attend(bc[0], bc[1], bc[2]).reshape(b, n_ctx, GQ_Q_COLS)
            oc_c = deltanet_out(oc_ctx, pc[9], dn_norm_g[l])
            xc = xc + mod_c[2] * merge_branches(pc[10], oa_c, ob_c, oc_c, w_br_a[l], w_br_b[l], w_br_c[l], w_o[l])

        hl = modulate(xl, norm2_g[l], mod_l[3], mod_l[4])
        xl = xl + mod_l[5] * conv_glu(hl, ffn_w1[l], ffn_conv_w[l], ffn_conv_b[l], ffn_w2[l])
        if not last:
            hc = modulate(xc, norm2_g[l], mod_c[3], mod_c[4])
            xc = xc + mod_c[5] * conv_glu(hc, ffn_w1[l], ffn_conv_w[l], ffn_conv_b[l], ffn_w2[l])
    return xl
```

```python
import math
from contextlib import ExitStack
import numpy as np
import concourse.bass as bass
import concourse.mybir as mybir
from concourse.bass_utils import run_bass_kernel_spmd

F32 = mybir.dt.float32
BF16 = mybir.dt.bfloat16
ALU = mybir.AluOpType
AF = mybir.ActivationFunctionType
AX = mybir.AxisListType

D = 1024
L = 4
NCTX = 256
NLAT = 4096
T = NCTX + NLAT
NIN = 7440
DFF = 2816
EPS = 1e-6
NO_SWDGE = True
TILES = [(0, 256)] + [(256 + 512 * i, 512) for i in range(8)]
NT128 = T // 128


class Buf:
    __slots__ = ("t", "w", "r", "dsem", "dcnt", "name")

    def __init__(self, t, name=""):
        self.t = t
        self.w = None
        self.r = []
        self.dsem = None
        self.dcnt = 0
        self.name = name

    def __getitem__(self, idx):
        return self.t[idx]


class Sched:
    def __init__(self, nc):
        self.nc = nc
        self.engs = {"pe": nc.tensor, "act": nc.scalar, "dve": nc.vector,
                     "pool": nc.gpsimd, "sp": nc.sync}
        self.sem = {}
        self.cnt = {}
        self.seen = {k: {} for k in self.engs}
        for k in self.engs:
            self.sem[k] = nc.alloc_semaphore(name="prog_" + k)
            self.cnt[k] = 0
        self.dbufs = []
        self.freesems = []
        self.ninst = 0
        self.nwaits = 0

    def _resolve(self, tok):
        if tok[0] == "e":
            return self.sem[tok[1]], tok[2], tok[1]
        b = tok[1]
        return b.dsem, 16 * b.dcnt, id(b.dsem)

    def _wait(self, en, toks):
        eng = self.engs[en]
        need = {}
        for tok in toks:
            if tok is None:
                continue
            if en == "pe" and tok[0] == "e" and tok[1] == "pe":
                continue
            if tok[0] == "d" and tok[1].dsem is None:
                continue
            sem, val, key = self._resolve(tok)
            if self.seen[en].get(key, 0) >= val:
                continue
            if key not in need or need[key][1] < val:
                need[key] = (sem, val)
        for key, (sem, val) in need.items():
            eng.wait_ge(sem, val)
            self.seen[en][key] = val
            self.nwaits += 1

    def _deps(self, reads, writes, nowaw=False):
        toks = []
        for b in reads:
            toks.append(b.w)
        for b in writes:
            if not nowaw:
                toks.append(b.w)
            toks.extend(b.r)
        return toks

    def _compact(self, toks):
        best = {}
        for t in toks:
            if t[0] == "e":
                k = t[1]
                if k not in best or best[k][2] < t[2]:
                    best[k] = t
            else:
                best[id(t[1])] = t
        return list(best.values())

    def _commit(self, tok, reads, writes):
        for b in reads:
            if b not in writes:
                b.r.append(tok)
                if len(b.r) > 16:
                    b.r = self._compact(b.r)
        for b in writes:
            b.w = tok
            b.r = []

    def op(self, en, fn, reads=(), writes=()):
        self._wait(en, self._deps(reads, writes))
        ins = fn(self.engs[en])
        self.cnt[en] += 1
        ins.then_inc(self.sem[en], 1)
        self._commit(("e", en, self.cnt[en]), reads, writes)
        self.ninst += 1
        return ins

    def mm(self, out_ap, pairs, reads=(), writes=(), **kw):
        self._wait("pe", self._deps(reads, writes))
        n = len(pairs)
        ins = None
        for i, (l, r) in enumerate(pairs):
            ins = self.nc.tensor.matmul(out_ap, l, r, start=(i == 0), stop=(i == n - 1), **kw)
        self.cnt["pe"] += 1
        ins.then_inc(self.sem["pe"], 1)
        self._commit(("e", "pe", self.cnt["pe"]), reads, writes)
        self.ninst += n
        return ins

    def mm1(self, out_ap, l, r, start, stop, reads=(), writes=(), **kw):
        self._wait("pe", self._deps(reads, writes))
        ins = self.nc.tensor.matmul(out_ap, l, r, start=start, stop=stop, **kw)
        self.cnt["pe"] += 1
        ins.then_inc(self.sem["pe"], 1)
        self._commit(("e", "pe", self.cnt["pe"]), reads, writes)
        self.ninst += 1
        return ins

    def transpose(self, out_ap, in_ap, ident_ap, reads=(), writes=()):
        self._wait("pe", self._deps(reads, writes))
        ins = self.nc.tensor.transpose(out_ap, in_ap, ident_ap)
        self.cnt["pe"] += 1
        ins.then_inc(self.sem["pe"], 1)
        self._commit(("e", "pe", self.cnt["pe"]), reads, writes)
        self.ninst += 1
        return ins

    def dma(self, q, out_ap, in_ap, reads=(), writes=(), **kw):
        if NO_SWDGE and q == "pool":
            q = "sp"
        assert len(writes) == 1
        dst = writes[0]
        toks = self._deps(reads, writes)
        if dst.w is not None and dst.w[0] == "d" and dst.w[1] is dst:
            toks = [t for t in toks if t is not dst.w]
        self._wait(q, toks)
        if dst.dsem is None:
            if self.freesems:
                dst.dsem, dst.dcnt = self.freesems.pop()
            else:
                self.nsem = getattr(self, "nsem", 0) + 1
                dst.dsem, dst.dcnt = self.nc.alloc_semaphore(name="dsem%d" % self.nsem), 0
            self.dbufs.append(dst)
        ins = self.engs[q].dma_start(out=out_ap, in_=in_ap, **kw)
        dst.dcnt += 1
        ins.then_inc(dst.dsem, 16)
        tok = ("d", dst)
        for b in reads:
            b.r.append(tok)
            if len(b.r) > 16:
                b.r = self._compact(b.r)
        dst.w = tok
        dst.r = []
        self.ninst += 1
        return ins

    def release_buf(self, b):
        if b.dsem is not None:
            self.freesems.append((b.dsem, b.dcnt))
            self.dbufs.remove(b)
            b.dsem = None

    def barrier(self):
        toks = [("e", k, self.cnt[k]) for k in self.engs if self.cnt[k] > 0]
        toks += [("d", b) for b in self.dbufs]
        for en in self.engs:
            self._wait(en, toks)

    def finish(self, bufs):
        self.barrier()


def lam_init_of(l):
    return 0.8 - 0.6 * math.exp(-0.3 * l)


class Prog:
    def __init__(self, nlayers=L, dbg=None, nb=4):
        self.nb = nb
        self.nl = nlayers
        self.dbg = dbg or set()
        nc = bass.Bass("TRN2", target_bir_lowering=False)
        self.nc = nc
        self.S = Sched(nc)
        self.inputs = {}
        self.outs = {}
        self.psum = [Buf(nc.alloc_psum_tensor("ps%d" % i, [128, 512], F32), "ps%d" % i) for i in range(8)]
        self.wq = 0

    def din(self, name, shape, dt=F32):
        t = self.nc.dram_tensor(name, list(shape), dt, kind="ExternalInput")
        b = Buf(t.ap(), name)
        self.inputs[name] = b
        return b

    def dscratch(self, name, shape, dt):
        kind = "ExternalOutput" if (name in self.dbg or name == "yT") else "Internal"
        t = self.nc.dram_tensor(name, list(shape), dt, kind=kind)
        b = Buf(t.ap(), name)
        if kind == "ExternalOutput":
            self.outs[name] = b
        return b

    def sb(self, es, name, shape, dt):
        self.uid = getattr(self, "uid", 0) + 1
        name = "%s_%d" % (name, self.uid)
        t = es.enter_context(self.nc.sbuf_tensor(name, list(shape), dt))
        b = Buf(t, name)
        es.callback(self.S.release_buf, b)
        return b

    def dmaq(self):
        self.wq += 1
        return "sp" if self.wq % 2 else "pool"
        return "sp" if self.wq % 2 else "pool"

    def declare(self):
        nl = self.nl
        di = self.din
        self.xT = di("xT", [4, D, T])
        self.cT = di("cT", [128, 4, 8, 2])
        self.w_mod = di("w_mod", [L, D, 6 * D])
        self.bmodT = di("bmodT", [128, L, 48])
        self.n1g = di("n1g", [128, L, 8])
        self.n2g = di("n2g", [128, L, 8])
        self.w_in = di("w_in", [L, D, NIN])
        self.qkg = di("qkg", [128, L, 4])
        self.ropec = di("ropec", [128, T])
        self.ropes = di("ropes", [128, T])
        self.lamT = di("lamT", [64, L, 4])
        self.sublnT = di("sublnT", [128, L])
        self.dnconvT = di("dnconvT", [128, L, 12, 3])
        self.dnalog = di("dnalog", [128, L, 8])
        self.dndtb = di("dndtb", [128, L, 8])
        self.dnng = di("dnng", [128, L])
        self.w_br = [di("w_br_a", [L, 512, D]), di("w_br_b", [L, 512, D]), di("w_br_c", [L, 512, D])]
        self.w_o = di("w_o", [L, D, D])
        self.ffn_w1 = di("ffn_w1", [L, D, 2 * DFF])
        self.fcwT = di("fcwT", [128, L, 22, 3])
        self.fcbT = di("fcbT", [128, L, 22])
        self.ffn_w2 = di("ffn_w2", [L, DFF, D])
        self.cst = di("cst", [128, 16, 128])
        ds = self.dscratch
        self.xs = ds("xs", [D, T], F32)
        self.daq = ds("daq", [4, 128, T], BF16)
        self.dak = ds("dak", [4, 128, T], BF16)
        self.dav = ds("dav", [T, 512], BF16)
        self.gqq = ds("gqq", [4, 128, T], BF16)
        self.gqk = ds("gqk", [2, 128, T], BF16)
        self.gqv = ds("gqv", [T, 128], BF16)
        self.dnq = ds("dnq", [4, 128, T], BF16)
        self.dnk = ds("dnk", [4, 128, T], BF16)
        self.dnv = ds("dnv", [4, 128, T], BF16)
        self.dng = ds("dng", [T, 8], F32)
        self.dnb = ds("dnb", [T, 8], F32)
        self.dnz = ds("dnz", [4, 128, T], BF16)
        self.gat = ds("gat", [24, 128, T], BF16)
        self.oa = ds("oa", [4, 128, T], BF16)
        self.ob = ds("ob", [4, 128, T], BF16)
        self.of = ds("of", [4, 128, T], F32)
        self.oc = ds("oc", [4, 128, T], BF16)
        self.gff = ds("gff", [22, 128, T], BF16)
        self.yT = ds("yT", [4, D, NLAT], F32)

    def build(self, stop=None):
        nc, S = self.nc, self.S
        self.declare()
        with ExitStack() as es0:
            self.C = self.sb(es0, "C", [128, 16, 128], F32)
            self.Cb = self.sb(es0, "Cb", [128, 16, 128], BF16)
            self.MOD = self.sb(es0, "MOD", [128, L, 48, 2], F32)
            self.GS = self.sb(es0, "GS", [128, L, 2, 8, 2], F32)
            self.small = {}
            for nm, src, shp in (("n1g", self.n1g, [128, L, 8]), ("n2g", self.n2g, [128, L, 8]),
                                 ("bmodT", self.bmodT, [128, L, 48]), ("qkg", self.qkg, [128, L, 4]),
                                 ("sublnT", self.sublnT, [128, L]), ("dnconvT", self.dnconvT, [128, L, 12, 3]),
                                 ("dnalog", self.dnalog, [128, L, 8]), ("dndtb", self.dndtb, [128, L, 8]),
                                 ("dnng", self.dnng, [128, L]), ("fcwT", self.fcwT, [128, L, 22, 3]),
                                 ("fcbT", self.fcbT, [128, L, 22]), ("cT", self.cT, [128, 4, 8, 2])):
                b = self.sb(es0, "s_" + nm, shp, F32)
                S.dma(self.dmaq(), b.t[:], src.t, [src], [b])
                self.small[nm] = b
            self.lam = self.sb(es0, "s_lam", [64, L, 4], F32)
            S.dma("sp", self.lam.t[:], self.lamT.t, [self.lamT], [self.lam])
            self.NLAM = self.sb(es0, "NLAM", [128, L], F32)
            S.dma("sp", self.C.t[:], self.cst.t, [self.cst], [self.C])
            S.op("dve", lambda e: e.tensor_copy(out=self.Cb.t[:], in_=self.C.t[:]), [self.C], [self.Cb])
            for b in range(self.nb):
                self.bi = b
                self.phase_mods()
                if stop == "mods":
                    return self.end(es0)
                S.dma("sp", self.xs.t, self.xT.t[b], [self.xT], [self.xs])
                for l in range(self.nl):
                    with ExitStack() as esh:
                        self.hT = self.sb(esh, "hT", [128, 8, T + 8], BF16)
                        self.phase_norm(l, 0)
                        self.phase_proj_a(l)
                        self.phase_proj_dn(l)
                    self.phase_attn(l)
                    self.phase_delta(l)
                    self.phase_merge(l)
                    with ExitStack() as esh:
                        self.hT = self.sb(esh, "hT", [128, 8, T + 8], BF16)
                        self.phase_norm(l, 1)
                        self.phase_ffn1(l)
                    self.phase_ffn2(l)
            self.end(es0)

    def end(self, es0):
        self.S.barrier()

    def cm(self, i, bf=False):
        return (self.Cb if bf else self.C).t[:, i, :]

    def phase_mods(self):
        nc, S = self.nc, self.S
        sm = self.small
        with ExitStack() as es:
            sc = self.sb(es, "sc", [128, 8, 2], F32)
            S.op("act", lambda e: e.activation(out=sc.t[:], in_=sm["cT"].t[:, self.bi], func=AF.Silu), [sm["cT"]], [sc])
            wst = [self.sb(es, "wm%d" % i, [128, 8, 512], F32) for i in range(2)]
            k = 0
            for l in range(self.nl):
                for g in range(12):
                    w = wst[k % 2]
                    k += 1
                    src = self.w_mod.t[l, :, g * 512:(g + 1) * 512].rearrange("(kc p) n -> p kc n", p=128)
                    S.dma(self.dmaq(), w.t[:], src, [self.w_mod], [w])
                    for s4 in range(4):
                        j = g * 4 + s4
                        ps = self.psum[j % 2]
                        S.mm(ps.t[:, 0:2], [(w.t[:, kc, s4 * 128:(s4 + 1) * 128], sc.t[:, kc, :]) for kc in range(8)],
                             [w, sc], [ps])
                        S.op("dve", lambda e, ps=ps, l=l, j=j: e.tensor_scalar(
                            out=self.MOD.t[:, l, j, :], in0=ps.t[:, 0:2], scalar1=sm["bmodT"].t[:, l, j:j + 1],
                            scalar2=None, op0=ALU.add), [ps, sm["bmodT"]], [self.MOD])
            for l in range(self.nl):
                for n, (gname, j0) in enumerate((("n1g", 8), ("n2g", 32))):
                    for s in range(2):
                        S.op("dve", lambda e, l=l, n=n, s=s, gname=gname, j0=j0: e.scalar_tensor_tensor(
                            out=self.GS.t[:, l, n, :, s], in0=self.MOD.t[:, l, j0:j0 + 8, s], scalar=1.0,
                            in1=sm[gname].t[:, l, :], op0=ALU.add, op1=ALU.mult), [self.MOD, sm[gname]], [self.GS])
            pr = self.sb(es, "lampr", [64, L, 2], F32)
            S.op("dve", lambda e: e.tensor_tensor(out=pr.t[:, :, 0], in0=self.lam.t[:, :, 0], in1=self.lam.t[:, :, 1], op=ALU.mult), [self.lam], [pr])
            S.op("dve", lambda e: e.tensor_tensor(out=pr.t[:, :, 1], in0=self.lam.t[:, :, 2], in1=self.lam.t[:, :, 3], op=ALU.mult), [self.lam], [pr])
            ps = self.psum[2]
            S.mm(ps.t[:, 0:2 * L], [(self.C.t[0:64, 1, :], pr.t[:].rearrange("p l s -> p (l s)"))], [self.C, pr], [ps])
            ex = self.sb(es, "lamex", [128, L, 2], F32)
            S.op("act", lambda e: e.activation(out=ex.t[:].rearrange("p l s -> p (l s)"), in_=ps.t[:, 0:2 * L], func=AF.Exp), [ps], [ex])
            for l in range(L):
                S.op("dve", lambda e, l=l: e.scalar_tensor_tensor(
                    out=self.NLAM.t[:, l:l + 1], in0=ex.t[:, l, 1:2], scalar=-lam_init_of(l), in1=ex.t[:, l, 0:1],
                    op0=ALU.add, op1=ALU.subtract), [ex], [self.NLAM])
            S.barrier()

    def phase_norm(self, l, n):
        nc, S = self.nc, self.S
        jshift = 0 if n == 0 else 24
        with ExitStack() as es:
            xt = [self.sb(es, "nx%d" % i, [128, 8, 512], F32) for i in range(2)]
            sq = [self.sb(es, "nsq%d" % i, [128, 8, 512], BF16) for i in range(2)]
            rs = [self.sb(es, "nrs%d" % i, [128, 512], F32) for i in range(2)]
            tmp = [self.sb(es, "ntmp%d" % i, [128, 512], F32) for i in range(3)]
            k = 0
            xsv = self.xs.t.rearrange("(kc p) t -> p kc t", p=128)
            for ti, (t0, n_) in enumerate(TILES):
                s = 1 if ti == 0 else 0
                x, q, r = xt[ti % 2], sq[ti % 2], rs[ti % 2]
                S.dma(self.dmaq(), x.t[:, :, 0:n_], xsv[:, :, t0:t0 + n_], [self.xs], [x])
                S.op("act", lambda e, x=x, q=q, n_=n_: e.activation(out=q.t[:, :, 0:n_], in_=x.t[:, :, 0:n_], func=AF.Square), [x], [q])
                ps = self.psum[ti % 2]
                S.mm(ps.t[:, 0:n_], [(self.cm(1, True), q.t[:, kc, 0:n_]) for kc in range(8)], [self.Cb, q], [ps])
                S.op("act", lambda e, ps=ps, r=r, n_=n_: e.activation(out=r.t[:, 0:n_], in_=ps.t[:, 0:n_], func=AF.Sqrt, scale=1.0 / D, bias=EPS), [ps], [r])
                S.op("dve", lambda e, r=r, n_=n_: e.reciprocal(out=r.t[:, 0:n_], in_=r.t[:, 0:n_]), [r], [r])
                for kc in range(8):
                    tm = tmp[k % 3]
                    k += 1
                    S.op("dve", lambda e, tm=tm, x=x, r=r, kc=kc, n_=n_, s=s: e.scalar_tensor_tensor(
                        out=tm.t[:, 0:n_], in0=x.t[:, kc, 0:n_], scalar=self.GS.t[:, l, n, kc, s:s + 1], in1=r.t[:, 0:n_],
                        op0=ALU.mult, op1=ALU.mult), [x, r, self.GS], [tm])
                    S.op("act", lambda e, tm=tm, kc=kc, n_=n_, t0=t0, s=s: e.activation(
                        out=self.hT.t[:, kc, t0:t0 + n_], in_=tm.t[:, 0:n_], func=AF.Identity,
                        bias=self.MOD.t[:, l, jshift + kc, s:s + 1]), [tm, self.MOD], [self.hT])
            S.barrier()

    def load_w(self, wst, wbf, src2d, ncols, kch=8, dup64=False):
        S = self.S
        S.dma("sp", wst.t[:, 0:kch, 0:ncols], src2d.rearrange("(kc p) n -> p kc n", p=128), [], [wst])
        if dup64:
            for i, (d0, s0) in enumerate(((0, 0), (64, 0), (128, 64), (192, 64))):
                S.op("pool", lambda e, d0=d0, s0=s0: e.tensor_copy(out=wbf.t[:, 0:kch, d0:d0 + 64], in_=wst.t[:, 0:kch, s0:s0 + 64]), [wst], [wbf])
        else:
            S.op("pool", lambda e: e.tensor_copy(out=wbf.t[:, 0:kch, 0:ncols], in_=wst.t[:, 0:kch, 0:ncols]), [wst], [wbf])

    def phase_proj_a(self, l):
        nc, S = self.nc, self.S
        sm = self.small
        with ExitStack() as es:
            wst = [self.sb(es, "pw%d" % i, [128, 8, 512], F32) for i in range(2)]
            wbf = [self.sb(es, "pwb%d" % i, [128, 8, 512], BF16) for i in range(2)]
            cosT = self.sb(es, "cosT", [128, T], F32)
            sinT = self.sb(es, "sinT", [128, T], F32)
            S.dma("sp", cosT.t[:], self.ropec.t, [self.ropec], [cosT])
            S.dma("sp", sinT.t[:], self.ropes.t, [self.ropes], [sinT])
            sqb = [self.sb(es, "psq%d" % i, [128, 512], BF16) for i in range(2)]
            rsb = [self.sb(es, "prs%d" % i, [128, 512], F32) for i in range(2)]
            xnb = [self.sb(es, "pxn%d" % i, [128, 512], F32) for i in range(2)]
            t1b = [self.sb(es, "pt1%d" % i, [128, 512], F32) for i in range(2)]
            t2b = [self.sb(es, "pt2%d" % i, [128, 512], F32) for i in range(2)]
            stg = [self.sb(es, "pst%d" % i, [128, 512], BF16) for i in range(3)]
            stf = [self.sb(es, "psf%d" % i, [128, 16], F32) for i in range(3)]
            nea = self.sb(es, "nea", [128, 8], F32)
            S.op("act", lambda e: e.activation(out=nea.t[:], in_=sm["dnalog"].t[:, l, :], func=AF.Exp), [sm["dnalog"]], [nea])
            st = {"w": 0, "u": 0, "s": 0}
            win = self.w_in.t

            def getw(col0, ncols, dup=False):
                i = st["w"] % 2
                st["w"] += 1
                self.load_w(wst[i], wbf[i], win[l, :, col0:col0 + ncols], ncols, dup64=dup)
                return wbf[i]

            def store(dst_buf, dst_ap, src_buf, src_ap):
                S.dma("pool", dst_ap, src_ap, [src_buf], [dst_buf])

            def fm(col0, nch, post, dup=False):
                wb = getw(col0, 128 if dup else nch * 128, dup)
                for j in range(nch):
                    for ti, (t0, n_) in enumerate(TILES):
                        u = st["u"]
                        st["u"] += 1
                        pq = self.psum[u % 2]
                        S.mm(pq.t[:, 0:n_], [(wb.t[:, kc, j * 128:(j + 1) * 128], self.hT.t[:, kc, t0:t0 + n_]) for kc in range(8)],
                             [wb, self.hT], [pq])
                        post(j, u, pq, t0, n_)

            def post_rope(dst, gi):
                def f(j, u, pq, t0, n_):
                    sq, rs, xn, t1, t2 = sqb[u % 2], rsb[u % 2], xnb[u % 2], t1b[u % 2], t2b[u % 2]
                    ps2, ps3 = self.psum[2 + u % 2], self.psum[4 + u % 2]
                    so = stg[st["s"] % 3]
                    st["s"] += 1
                    S.op("act", lambda e: e.activation(out=sq.t[:, 0:n_], in_=pq.t[:, 0:n_], func=AF.Square), [pq], [sq])
                    S.mm(ps2.t[:, 0:n_], [(self.cm(2, True), sq.t[:, 0:n_])], [self.Cb, sq], [ps2])
                    S.op("act", lambda e: e.activation(out=rs.t[:, 0:n_], in_=ps2.t[:, 0:n_], func=AF.Sqrt, scale=1.0 / 64, bias=EPS), [ps2], [rs])
                    S.op("dve", lambda e: e.reciprocal(out=rs.t[:, 0:n_], in_=rs.t[:, 0:n_]), [rs], [rs])
                    S.op("dve", lambda e: e.scalar_tensor_tensor(out=xn.t[:, 0:n_], in0=pq.t[:, 0:n_], scalar=sm["qkg"].t[:, l, gi:gi + 1],
                                                                  in1=rs.t[:, 0:n_], op0=ALU.mult, op1=ALU.mult), [pq, rs, sm["qkg"]], [xn])
                    S.mm(ps3.t[:, 0:n_], [(self.cm(3), xn.t[:, 0:n_])], [self.C, xn], [ps3])
                    S.op("pool", lambda e: e.tensor_tensor(out=t1.t[:, 0:n_], in0=xn.t[:, 0:n_], in1=cosT.t[:, t0:t0 + n_], op=ALU.mult), [xn, cosT], [t1])
                    S.op("dve", lambda e: e.tensor_tensor(out=t2.t[:, 0:n_], in0=ps3.t[:, 0:n_], in1=sinT.t[:, t0:t0 + n_], op=ALU.mult), [ps3, sinT], [t2])
                    S.op("pool", lambda e: e.tensor_tensor(out=so.t[:, 0:n_], in0=t1.t[:, 0:n_], in1=t2.t[:, 0:n_], op=ALU.add), [t1, t2], [so])
                    store(dst, dst.t[j, :, t0:t0 + n_], so, so.t[:, 0:n_])
                return f

            def post_act(dst, func, j0=0):
                def f(j, u, pq, t0, n_):
                    so = stg[st["s"] % 3]
                    st["s"] += 1
                    S.op("act", lambda e: e.activation(out=so.t[:, 0:n_], in_=pq.t[:, 0:n_], func=func), [pq], [so])
                    store(dst, dst.t[j0 + j, :, t0:t0 + n_], so, so.t[:, 0:n_])
                return f

            def tm(col0, ncols, post):
                wb = getw(col0, ncols)
                for tb in range(NT128):
                    u = st["u"]
                    st["u"] += 1
                    pq = self.psum[u % 2]
                    S.mm(pq.t[:, 0:ncols], [(self.hT.t[:, kc, tb * 128:(tb + 1) * 128], wb.t[:, kc, 0:ncols]) for kc in range(8)],
                         [wb, self.hT], [pq])
                    post(tb, u, pq)

            def post_copy(dst, ncols):
                def f(tb, u, pq):
                    so = stg[st["s"] % 3]
                    st["s"] += 1
                    eng = "dve" if u % 2 else "act"
                    if eng == "dve":
                        S.op("dve", lambda e: e.tensor_copy(out=so.t[:, 0:ncols], in_=pq.t[:, 0:ncols]), [pq], [so])
                    else:
                        S.op("act", lambda e: e.activation(out=so.t[:, 0:ncols], in_=pq.t[:, 0:ncols], func=AF.Copy), [pq], [so])
                    store(dst, dst.t[tb * 128:(tb + 1) * 128, :], so, so.t[:, 0:ncols])
                return f

            def post_ba(tb, u, pq):
                so = stf[st["s"] % 3]
                st["s"] += 1
                S.op("act", lambda e: e.activation(out=so.t[:, 0:8], in_=pq.t[:, 0:8], func=AF.Sigmoid), [pq], [so])
                S.op("dve", lambda e: e.tensor_tensor(out=so.t[:, 8:16], in0=pq.t[:, 8:16], in1=sm["dndtb"].t[:, l, :], op=ALU.add), [pq, sm["dndtb"]], [so])
                S.op("act", lambda e: e.activation(out=so.t[:, 8:16], in_=so.t[:, 8:16], func=AF.Exp), [so], [so])
                S.op("act", lambda e: e.activation(out=so.t[:, 8:16], in_=so.t[:, 8:16], func=AF.Ln, bias=1.0), [so], [so])
                S.op("dve", lambda e: e.scalar_tensor_tensor(out=so.t[:, 8:16], in0=so.t[:, 8:16], scalar=-1.0, in1=nea.t[:], op0=ALU.mult, op1=ALU.mult), [so, nea], [so])
                store(self.dnb, self.dnb.t[tb * 128:(tb + 1) * 128, :], so, so.t[:, 0:8])
                store(self.dng, self.dng.t[tb * 128:(tb + 1) * 128, :], so, so.t[:, 8:16])

            fm(0, 4, post_rope(self.daq, 0))
            fm(512, 4, post_rope(self.dak, 1))
            tm(1024, 512, post_copy(self.dav, 512))
            fm(1536, 4, post_rope(self.gqq, 2))
            fm(2048, 2, post_rope(self.gqk, 3), dup=True)
            tm(2176, 128, post_copy(self.gqv, 128))
            tm(3840, 16, post_ba)
            fm(3856, 4, post_act(self.dnz, AF.Silu))
            for g in range(6):
                fm(4368 + 512 * g, 4, post_act(self.gat, AF.Sigmoid, 4 * g))
            S.barrier()

    def phase_proj_dn(self, l):
        nc, S = self.nc, self.S
        sm = self.small
        W = T + 4
        with ExitStack() as es:
            wst = [self.sb(es, "dw%d" % i, [128, 8, 128], F32) for i in range(2)]
            wbf = [self.sb(es, "dwb%d" % i, [128, 8, 128], BF16) for i in range(2)]
            RB = [self.sb(es, "RB%d" % i, [128, W], F32) for i in range(2)]
            cv = self.sb(es, "dcv", [128, W], F32)
            sqb = self.sb(es, "dsq", [128, W], BF16)
            rsb = [self.sb(es, "drs%d" % i, [128, 512], F32) for i in range(2)]
            stg = [self.sb(es, "dst%d" % i, [128, 512], BF16) for i in range(3)]
            for rb in RB:
                S.op("pool", lambda e, rb=rb: e.memset(rb.t[:], 0.0), [], [rb])
            pieces = [(0, 0, 256)] + [(258 + 512 * i, 256 + 512 * i, 512) for i in range(8)]
            u = 0
            sidx = 0
            for c in range(12):
                rb = RB[c % 2]
                self.load_w(wst[c % 2], wbf[c % 2], self.w_in.t[l, :, 2304 + c * 128:2304 + (c + 1) * 128], 128)
                wb = wbf[c % 2]
                for ti, (t0, n_) in enumerate(TILES):
                    pq = self.psum[u % 2]
                    u += 1
                    S.mm(pq.t[:, 0:n_], [(wb.t[:, kc, :], self.hT.t[:, kc, t0:t0 + n_]) for kc in range(8)], [wb, self.hT], [pq])
                    pos = 1 + t0 if ti == 0 else 3 + t0
                    if ti % 2:
                        S.op("dve", lambda e, pq=pq, pos=pos, n_=n_: e.tensor_copy(out=rb.t[:, pos:pos + n_], in_=pq.t[:, 0:n_]), [pq], [rb])
                    else:
                        S.op("act", lambda e, pq=pq, pos=pos, n_=n_: e.activation(out=rb.t[:, pos:pos + n_], in_=pq.t[:, 0:n_], func=AF.Copy), [pq], [rb])
                cw = sm["dnconvT"].t
                n2 = W - 2
                S.op("dve", lambda e: e.tensor_scalar(out=cv.t[:, 0:n2], in0=rb.t[:, 0:n2], scalar1=cw[:, l, c, 0:1], scalar2=None, op0=ALU.mult), [rb, sm["dnconvT"]], [cv])
                S.op("dve", lambda e: e.scalar_tensor_tensor(out=cv.t[:, 0:n2], in0=rb.t[:, 1:n2 + 1], scalar=cw[:, l, c, 1:2], in1=cv.t[:, 0:n2], op0=ALU.mult, op1=ALU.add), [rb, cv, sm["dnconvT"]], [cv])
                S.op("dve", lambda e: e.scalar_tensor_tensor(out=cv.t[:, 0:n2], in0=rb.t[:, 2:n2 + 2], scalar=cw[:, l, c, 2:3], in1=cv.t[:, 0:n2], op0=ALU.mult, op1=ALU.add), [rb, cv, sm["dnconvT"]], [cv])
                S.op("act", lambda e: e.activation(out=cv.t[:, 0:n2], in_=cv.t[:, 0:n2], func=AF.Silu), [cv], [cv])
                dst = (self.dnq, self.dnk, self.dnv)[c // 4]
                hh = c % 4
                if c < 8:
                    S.op("act", lambda e: e.activation(out=sqb.t[:, 0:n2], in_=cv.t[:, 0:n2], func=AF.Square), [cv], [sqb])
                for (ci, tk, n_) in pieces:
                    so = stg[sidx % 3]
                    sidx += 1
                    if c < 8:
                        ps2 = self.psum[2 + sidx % 2]
                        rs = rsb[sidx % 2]
                        S.mm(ps2.t[:, 0:n_], [(self.cm(1, True), sqb.t[:, ci:ci + n_])], [self.Cb, sqb], [ps2])
                        S.op("act", lambda e, rs=rs, ps2=ps2, n_=n_: e.activation(out=rs.t[:, 0:n_], in_=ps2.t[:, 0:n_], func=AF.Sqrt, bias=EPS), [ps2], [rs])
                        S.op("dve", lambda e, rs=rs, n_=n_: e.reciprocal(out=rs.t[:, 0:n_], in_=rs.t[:, 0:n_]), [rs], [rs])
                        S.op("pool", lambda e, rs=rs, so=so, ci=ci, n_=n_: e.tensor_tensor(out=so.t[:, 0:n_], in0=cv.t[:, ci:ci + n_], in1=rs.t[:, 0:n_], op=ALU.mult), [cv, rs], [so])
                    else:
                        S.op("pool", lambda e, so=so, ci=ci, n_=n_: e.tensor_copy(out=so.t[:, 0:n_], in_=cv.t[:, ci:ci + n_]), [cv], [so])
                    S.dma("pool", dst.t[hh, :, tk:tk + n_], so.t[:, 0:n_], [so], [dst])
            S.barrier()


    def phase_attn(self, l):
        nc, S = self.nc, self.S
        sm = self.small
        last = (l == L - 1)
        li = lam_init_of(l)
        with ExitStack() as es:
            Kda = self.sb(es, "Kda", [128, 4, T], BF16)
            Kgq = self.sb(es, "Kgq", [128, 2, T], BF16)
            Vda = self.sb(es, "Vda", [128, NT128, 512], BF16)
            Vgq = self.sb(es, "Vgq", [128, NT128, 128], BF16)
            for h in range(4):
                S.dma("sp", Kda.t[:, h, :], self.dak.t[h], [self.dak], [Kda])
            for h in range(2):
                S.dma("sp", Kgq.t[:, h, :], self.gqk.t[h], [self.gqk], [Kgq])
            S.dma("sp", Vda.t[:], self.dav.t.rearrange("(c p) n -> p c n", p=128), [self.dav], [Vda])
            S.dma("sp", Vgq.t[:], self.gqv.t.rearrange("(c p) n -> p c n", p=128), [self.gqv], [Vgq])
            Qda = [self.sb(es, "Qda%d" % i, [128, 4, 512], BF16) for i in range(2)]
            Qgq = [self.sb(es, "Qgq%d" % i, [128, 4, 512], BF16) for i in range(2)]
            PT = [self.sb(es, "PT%d" % i, [128, 512], BF16) for i in range(4)]
            ftmp = [self.sb(es, "af%d" % i, [128, 512], F32) for i in range(6)]
            sqb = self.sb(es, "asq", [128, 512], BF16)
            stg = [self.sb(es, "ast%d" % i, [128, 512], BF16) for i in range(3)]
            sctr = [0]
            pctr = [0]
            ones_b = self.cm(1, True)
            SB = self.psum[0:4]
            OB = self.psum[4:8]

            def run_unit(kT, qT, vsel, mrows, n_, nkc):
                def qk(kc):
                    res = []
                    for sub in range(2):
                        sbk = SB[sctr[0] % 4]
                        sctr[0] += 1
                        ka, kb_ = kT(sub, kc)
                        qa, qb_ = qT(sub)
                        S.mm(sbk.t[:, 0:n_], [(ka, qa)], [kb_, qb_], [sbk])
                        pt = PT[pctr[0] % 4]
                        pctr[0] += 1
                        S.op("act", lambda e, pt=pt, sbk=sbk: e.activation(out=pt.t[:, 0:n_], in_=sbk.t[:, 0:n_], func=AF.Exp, scale=0.125), [sbk], [pt])
                        res.append(pt)
                    return res

                def pv(kc, pts):
                    for sub in range(2):
                        pt = pts[sub]
                        va, vb_ = vsel(sub, kc)
                        p0 = 0 if mrows == 128 else 64 * sub
                        S.mm1(OB[2 * sub].t[p0:p0 + mrows, 0:n_], va, pt.t[:, 0:n_], kc == 0, kc == nkc - 1, [vb_, pt], [OB[2 * sub]])
                        S.mm1(OB[2 * sub + 1].t[p0:p0 + mrows, 0:n_], ones_b[:, 0:mrows], pt.t[:, 0:n_], kc == 0, kc == nkc - 1, [self.Cb, pt], [OB[2 * sub + 1]])
                prev = qk(0)
                for kc in range(1, nkc):
                    cur = qk(kc)
                    pv(kc - 1, prev)
                    prev = cur
                pv(nkc - 1, prev)

            for ti, (t0, n_) in enumerate(TILES):
                if ti == 0 and last:
                    continue
                nkc = 2 if ti == 0 else NT128
                qd, qg = Qda[ti % 2], Qgq[ti % 2]
                for h in range(4):
                    S.dma("sp", qd.t[:, h, 0:n_], self.daq.t[h, :, t0:t0 + n_], [self.daq], [qd])
                    S.dma("sp", qg.t[:, h, 0:n_], self.gqq.t[h, :, t0:t0 + n_], [self.gqq], [qg])
                for h in range(4):
                    run_unit(lambda sub, kc: (Kda.t[64 * sub:64 * sub + 64, h, kc * 128:(kc + 1) * 128], Kda),
                             lambda sub: (qd.t[64 * sub:64 * sub + 64, h, 0:n_], qd),
                             lambda sub, kc: (Vda.t[:, kc, h * 128:(h + 1) * 128], Vda), 128, n_, nkc)
                    r1, a1, r2, a2, dd, rs = ftmp
                    S.op("dve", lambda e: e.reciprocal(out=r1.t[:, 0:n_], in_=OB[1].t[:, 0:n_]), [OB[1]], [r1])
                    S.op("dve", lambda e: e.tensor_tensor(out=a1.t[:, 0:n_], in0=OB[0].t[:, 0:n_], in1=r1.t[:, 0:n_], op=ALU.mult), [OB[0], r1], [a1])
                    S.op("dve", lambda e: e.reciprocal(out=r2.t[:, 0:n_], in_=OB[3].t[:, 0:n_]), [OB[3]], [r2])
                    S.op("dve", lambda e: e.tensor_tensor(out=a2.t[:, 0:n_], in0=OB[2].t[:, 0:n_], in1=r2.t[:, 0:n_], op=ALU.mult), [OB[2], r2], [a2])
                    S.op("dve", lambda e: e.scalar_tensor_tensor(out=dd.t[:, 0:n_], in0=a2.t[:, 0:n_], scalar=self.NLAM.t[:, l:l + 1], in1=a1.t[:, 0:n_],
                                                                  op0=ALU.mult, op1=ALU.add), [a2, a1, self.NLAM], [dd])
                    S.op("act", lambda e: e.activation(out=sqb.t[:, 0:n_], in_=dd.t[:, 0:n_], func=AF.Square), [dd], [sqb])
                    pss = SB[sctr[0] % 4]
                    sctr[0] += 1
                    S.mm(pss.t[:, 0:n_], [(ones_b, sqb.t[:, 0:n_])], [self.Cb, sqb], [pss])
                    f = 1.0 / ((1.0 - li) ** 2)
                    S.op("act", lambda e: e.activation(out=rs.t[:, 0:n_], in_=pss.t[:, 0:n_], func=AF.Sqrt, scale=f / 128.0, bias=EPS * f), [pss], [rs])
                    S.op("dve", lambda e: e.reciprocal(out=rs.t[:, 0:n_], in_=rs.t[:, 0:n_]), [rs], [rs])
                    so = stg[(ti * 8 + h) % 3]
                    S.op("dve", lambda e: e.scalar_tensor_tensor(out=so.t[:, 0:n_], in0=dd.t[:, 0:n_], scalar=sm["sublnT"].t[:, l:l + 1], in1=rs.t[:, 0:n_],
                                                                  op0=ALU.mult, op1=ALU.mult), [dd, rs, sm["sublnT"]], [so])
                    S.dma("pool", self.oa.t[h, :, t0:t0 + n_], so.t[:, 0:n_], [so], [self.oa])
                for c in range(4):
                    kv = c // 2
                    run_unit(lambda sub, kc: (Kgq.t[64 * sub:64 * sub + 64, kv, kc * 128:(kc + 1) * 128], Kgq),
                             lambda sub: (qg.t[64 * sub:64 * sub + 64, c, 0:n_], qg),
                             lambda sub, kc: (Vgq.t[:, kc, kv * 64:(kv + 1) * 64], Vgq), 64, n_, nkc)
                    so = stg[(ti * 8 + 4 + c) % 3]
                    r1 = ftmp[0]
                    for sub in range(2):
                        p0 = 64 * sub
                        S.op("dve", lambda e: e.reciprocal(out=r1.t[p0:p0 + 64, 0:n_], in_=OB[2 * sub + 1].t[p0:p0 + 64, 0:n_]), [OB[2 * sub + 1]], [r1])
                        S.op("dve", lambda e: e.tensor_tensor(out=so.t[p0:p0 + 64, 0:n_], in0=OB[2 * sub].t[p0:p0 + 64, 0:n_], in1=r1.t[p0:p0 + 64, 0:n_], op=ALU.mult), [OB[2 * sub], r1], [so])
                    S.dma("pool", self.ob.t[c, :, t0:t0 + n_], so.t[:, 0:n_], [so], [self.ob])
            S.barrier()


    def phase_delta(self, l):
        nc, S = self.nc, self.S
        sm = self.small
        C = self.C
        SC = 128.0 ** -0.5
        H4 = [128, 4, 128]
        with ExitStack() as es:
            G = self.sb(es, "dG", [128, NT128, 8], F32)
            Bt = self.sb(es, "dB", [128, NT128, 8], F32)
            NB = self.sb(es, "dNB", [128, NT128, 8], F32)
            S.dma("sp", G.t[:], self.dng.t.rearrange("(c p) n -> p c n", p=128), [self.dng], [G])
            S.dma("sp", Bt.t[:], self.dnb.t.rearrange("(c p) n -> p c n", p=128), [self.dnb], [Bt])
            S.op("dve", lambda e: e.tensor_scalar(out=NB.t[:], in0=Bt.t[:], scalar1=-1.0, scalar2=None, op0=ALU.mult), [Bt], [NB])
            qB = [self.sb(es, "dq%d" % i, [128, 4, 512], BF16) for i in range(2)]
            kB = [self.sb(es, "dk%d" % i, [128, 4, 512], BF16) for i in range(2)]
            vB = [self.sb(es, "dv%d" % i, [128, 4, 512], BF16) for i in range(2)]
            def w(name, dt=F32, n=1):
                r = [self.sb(es, "%s%d" % (name, i), H4, dt) for i in range(n)]
                return r if n > 1 else r[0]
            EG = self.sb(es, "dEG", [128, 8], F32)
            EGL = self.sb(es, "dEGL", [128, 2, 4], F32)
            GL, decI, decS, Tacc, Vtok, Ktok, Kg, Ktail, ub, wT, qk, qd, egb, vnew = [w(n_) for n_ in
                ("dGL", "ddecI", "ddecS", "dTacc", "dVtok", "dKtok", "dKg", "dKtail", "dub", "dwT", "dqk", "dqd", "degb", "dvnew")]
            Pm = w("dP", F32, 2)
            Ptm = w("dPt", F32, 2)
            ofl, osum, rsn, onrm = w("dofl"), w("dosum"), w("drsn"), w("donrm")
            sqn = w("dsqn", BF16)
            zt = w("dzt", BF16)
            ocst = w("docst", BF16)
            ofst = w("dofst")
            Sst = self.sb(es, "dS", H4, F32)
            b0, b1, b2, b3, b4, b5, b6, b7 = self.psum

            def bc_f(ap2d, P=128):
                return ap2d.unsqueeze(2).to_broadcast([P, 4, 128])

            def bc_h(ap2d):
                return ap2d.unsqueeze(1).to_broadcast(H4)

            def v3(ps):
                return ps.t[:, :].rearrange("p (h n) -> p h n", h=4)

            def tile_local(d, tl, qb, kb_, vb, off):
                jd = 4 * d
                g4 = G.t[:, tl, jd:jd + 4]
                S.mm(b0.t[:, 0:4], [(C.t[:, 4 + d, :], g4)], [C, G], [b0])
                S.mm(b0.t[:, 4:8], [(C.t[:, 6 + d, :], g4)], [C, G], [b0])
                S.mm(b0.t[:, 8:12], [(C.t[0:64, 1, :], G.t[0:64, tl, jd:jd + 4])], [C, G], [b0])
                S.mm(b0.t[:, 12:16], [(C.t[64:128, 1, :], G.t[64:128, tl, jd:jd + 4])], [C, G], [b0])
                S.op("act", lambda e: e.activation(out=EG.t[:], in_=b0.t[:, 0:8], func=AF.Exp), [b0], [EG])
                S.op("act", lambda e: e.activation(out=EGL.t[:].rearrange("p c h -> p (c h)"), in_=b0.t[:, 8:16], func=AF.Exp), [b0], [EGL])
                S.op("pool", lambda e: e.tensor_tensor(out=GL.t[:], in0=bc_h(C.t[:, 4 + d, :]), in1=bc_f(g4), op=ALU.mult), [C, G], [GL])
                for h in range(4):
                    kt = kb_.t[:, h, off:off + 128]
                    hs = slice(h * 128, (h + 1) * 128)
                    S.mm(b1.t[:, hs], [(kt, kt)], [kb_], [b1])
                    S.mm(b2.t[:, hs], [(kt, qb.t[:, h, off:off + 128])], [kb_, qb], [b2])
                    S.mm(b3.t[:, hs], [(C.t[:, 1, :], GL.t[:, h, :]), (GL.t[:, h, :], C.t[:, 8, :]), (C.t[:, 0, :], C.t[:, 9 + d, :])], [C, GL], [b3])
                S.op("act", lambda e: e.activation(out=decI.t[:].rearrange("p h n -> p (h n)"), in_=b3.t[:, :], func=AF.Exp), [b3], [decI])
                S.op("pool", lambda e: e.tensor_tensor(out=decS.t[:], in0=decI.t[:], in1=bc_h(C.t[:, 11 + d, :]), op=ALU.mult), [decI, C], [decS])
                for h in range(4):
                    S.mm(b3.t[:, h * 128:(h + 1) * 128], [(C.t[:, 1, :], GL.t[:, h, :])], [C, GL], [b3])
                S.op("act", lambda e: e.activation(out=egb.t[:].rearrange("p h n -> p (h n)"), in_=b3.t[:, :], func=AF.Exp), [b3], [egb])
                pt, pp = Ptm[0], Pm[0]
                S.op("dve", lambda e: e.tensor_tensor(out=pt.t[:], in0=v3(b1), in1=decS.t[:], op=ALU.mult), [b1, decS], [pt])
                S.op("pool", lambda e: e.tensor_tensor(out=pt.t[:], in0=pt.t[:], in1=bc_f(NB.t[:, tl, jd:jd + 4]), op=ALU.mult), [pt, NB], [pt])
                S.op("pool", lambda e: e.tensor_tensor(out=Tacc.t[:], in0=pt.t[:], in1=bc_h(C.t[:, 0, :]), op=ALU.add), [pt, C], [Tacc])
                for h in range(4):
                    S.mm(b4.t[:, h * 128:(h + 1) * 128], [(pt.t[:, h, :], C.t[:, 0, :])], [pt, C], [b4])
                S.op("act", lambda e: e.activation(out=pp.t[:].rearrange("p h n -> p (h n)"), in_=b4.t[:, :], func=AF.Copy), [b4], [pp])
                for k in range(1, 6):
                    pt0, pp0 = Ptm[(k - 1) % 2], Pm[(k - 1) % 2]
                    pt1, pp1 = Ptm[k % 2], Pm[k % 2]
                    for h in range(4):
                        S.mm(b4.t[:, h * 128:(h + 1) * 128], [(pt0.t[:, h, :], pp0.t[:, h, :])], [pt0, pp0], [b4])
                    if k < 5:
                        for h in range(4):
                            S.mm(b5.t[:, h * 128:(h + 1) * 128], [(pp0.t[:, h, :], pt0.t[:, h, :])], [pt0, pp0], [b5])
                    S.op("act", lambda e: e.activation(out=pp1.t[:].rearrange("p h n -> p (h n)"), in_=b4.t[:, :], func=AF.Copy), [b4], [pp1])
                    if k < 5:
                        S.op("dve", lambda e: e.tensor_copy(out=pt1.t[:].rearrange("p h n -> p (h n)"), in_=b5.t[:, :]), [b5], [pt1])
                    for h in range(4):
                        S.mm(b6.t[:, h * 128:(h + 1) * 128], [(pp1.t[:, h, :], Tacc.t[:, h, :])], [pp1, Tacc], [b6])
                    S.op("dve", lambda e: e.tensor_tensor(out=Tacc.t[:], in0=v3(b6), in1=Tacc.t[:], op=ALU.add), [b6, Tacc], [Tacc])
                for h in range(4):
                    S.mm(b5.t[:, h * 128:(h + 1) * 128], [(vb.t[:, h, off:off + 128], self.cm(0, True))], [vb, self.Cb], [b5])
                S.op("act", lambda e: e.activation(out=Vtok.t[:].rearrange("p h n -> p (h n)"), in_=b5.t[:, :], func=AF.Copy), [b5], [Vtok])
                for h in range(4):
                    S.mm(b4.t[:, h * 128:(h + 1) * 128], [(kb_.t[:, h, off:off + 128], self.cm(0, True))], [kb_, self.Cb], [b4])
                S.op("dve", lambda e: e.tensor_tensor(out=Kg.t[:], in0=v3(b4), in1=bc_f(EG.t[:, 0:4]), op=ALU.mult), [b4, EG], [Kg])
                S.op("dve", lambda e: e.tensor_tensor(out=Ktail.t[:], in0=v3(b4), in1=bc_f(EG.t[:, 4:8]), op=ALU.mult), [b4, EG], [Ktail])
                for h in range(4):
                    S.mm(b1.t[:, h * 128:(h + 1) * 128], [(Tacc.t[:, h, :], Vtok.t[:, h, :])], [Tacc, Vtok], [b1])
                S.op("dve", lambda e: e.tensor_tensor(out=ub.t[:], in0=v3(b1), in1=bc_f(Bt.t[:, tl, jd:jd + 4]), op=ALU.mult), [b1, Bt], [ub])
                for h in range(4):
                    S.mm(b6.t[:, h * 128:(h + 1) * 128], [(Kg.t[:, h, :], Tacc.t[:, h, :])], [Kg, Tacc], [b6])
                S.op("act", lambda e: e.activation(out=wT.t[:].rearrange("p h n -> p (h n)"), in_=b6.t[:, :], func=AF.Copy), [b6], [wT])
                S.op("dve", lambda e: e.scalar_tensor_tensor(out=qk.t[:].rearrange("p h n -> p (h n)"), in0=b2.t[:, :], scalar=SC,
                                                              in1=decI.t[:].rearrange("p h n -> p (h n)"), op0=ALU.mult, op1=ALU.mult), [b2, decI], [qk])
                for h in range(4):
                    S.op("dve", lambda e, h=h: e.scalar_tensor_tensor(out=qd.t[:, h, :], in0=qb.t[:, h, off:off + 128], scalar=SC, in1=egb.t[:, h, :],
                                                                      op0=ALU.mult, op1=ALU.mult), [qb, egb], [qd])

            def scan_step(d, tl, ch):
                jd = 4 * d
                r0 = 64 * ch
                rows = slice(r0, r0 + 64)
                for h in range(4):
                    S.mm(b0.t[rows, h * 128:(h + 1) * 128], [(wT.t[:, h, rows], Sst.t[:, h, :])], [wT, Sst], [b0])
                S.op("dve", lambda e: e.tensor_tensor(out=vnew.t[rows], in0=b0.t[rows, :].rearrange("p (h n) -> p h n", h=4),
                                                      in1=bc_f(NB.t[rows, tl, jd:jd + 4], 64), op=ALU.mult), [b0, NB], [vnew])
                S.op("pool", lambda e: e.tensor_tensor(out=vnew.t[rows], in0=vnew.t[rows], in1=ub.t[rows], op=ALU.add), [vnew, ub], [vnew])
                for h in range(4):
                    oc_ = slice(h * 128 + r0, h * 128 + r0 + 64)
                    S.mm(b7.t[:, oc_], [(Sst.t[:, h, :], qd.t[:, h, rows]), (vnew.t[rows, h, :], qk.t[rows, h, rows])], [Sst, qd, vnew, qk], [b7])
                for h in range(4):
                    S.mm(b3.t[:, h * 128:(h + 1) * 128], [(Ktail.t[rows, h, :], vnew.t[rows, h, :])], [Ktail, vnew], [b3])
                S.op("dve", lambda e: e.tensor_tensor(out=Sst.t[:], in0=Sst.t[:], in1=bc_f(EGL.t[:, ch, :]), op=ALU.mult), [Sst, EGL], [Sst])
                S.op("dve", lambda e: e.tensor_tensor(out=Sst.t[:], in0=v3(b3), in1=Sst.t[:], op=ALU.add), [b3, Sst], [Sst])

            for d in range(2):
                S.op("pool", lambda e: e.memset(Sst.t[:], 0.0), [], [Sst])
                blocks = list(range(len(TILES)))
                if d == 1:
                    blocks = [0] + blocks[:0:-1]
                for bi, blk in enumerate(blocks):
                    t0, n_ = TILES[blk]
                    qb, kb_, vb = qB[bi % 2], kB[bi % 2], vB[bi % 2]
                    for h in range(4):
                        S.dma("sp", qb.t[:, h, 0:n_], self.dnq.t[h, :, t0:t0 + n_], [self.dnq], [qb])
                        S.dma("sp", kb_.t[:, h, 0:n_], self.dnk.t[h, :, t0:t0 + n_], [self.dnk], [kb_])
                        S.dma("sp", vb.t[:, h, 0:n_], self.dnv.t[h, :, t0:t0 + n_], [self.dnv], [vb])
                    subs = list(range(n_ // 128))
                    if d == 1:
                        subs = subs[::-1]
                    for si in subs:
                        tk = t0 + si * 128
                        tl = tk // 128
                        tile_local(d, tl, qb, kb_, vb, si * 128)
                        for ch in ((0, 1) if d == 0 else (1, 0)):
                            scan_step(d, tl, ch)
                        ofv = self.of.t[:, :, tk:tk + 128].rearrange("h p n -> p h n")
                        if d == 0:
                            S.op("act", lambda e: e.activation(out=ofst.t[:].rearrange("p h n -> p (h n)"), in_=b7.t[:, :], func=AF.Copy), [b7], [ofst])
                            S.dma("pool", ofv, ofst.t[:], [ofst], [self.of])
                        else:
                            S.dma("sp", ofl.t[:], ofv, [self.of], [ofl])
                            S.dma("sp", zt.t[:], self.dnz.t[:, :, tk:tk + 128].rearrange("h p n -> p h n"), [self.dnz], [zt])
                            S.op("dve", lambda e: e.tensor_tensor(out=osum.t[:], in0=v3(b7), in1=ofl.t[:], op=ALU.add), [b7, ofl], [osum])
                            S.op("act", lambda e: e.activation(out=sqn.t[:], in_=osum.t[:], func=AF.Square), [osum], [sqn])
                            S.mm(b2.t[:, :], [(self.cm(1, True), sqn.t[:].rearrange("p h n -> p (h n)"))], [self.Cb, sqn], [b2])
                            S.op("act", lambda e: e.activation(out=rsn.t[:].rearrange("p h n -> p (h n)"), in_=b2.t[:, :], func=AF.Sqrt, scale=1.0 / 128, bias=EPS), [b2], [rsn])
                            S.op("dve", lambda e: e.reciprocal(out=rsn.t[:], in_=rsn.t[:]), [rsn], [rsn])
                            S.op("dve", lambda e: e.scalar_tensor_tensor(out=onrm.t[:].rearrange("p h n -> p (h n)"), in0=osum.t[:].rearrange("p h n -> p (h n)"),
                                                                          scalar=sm["dnng"].t[:, l:l + 1], in1=rsn.t[:].rearrange("p h n -> p (h n)"),
                                                                          op0=ALU.mult, op1=ALU.mult), [osum, rsn, sm["dnng"]], [onrm])
                            S.op("pool", lambda e: e.tensor_tensor(out=ocst.t[:], in0=onrm.t[:], in1=zt.t[:], op=ALU.mult), [onrm, zt], [ocst])
                            S.dma("pool", self.oc.t[:, :, tk:tk + 128].rearrange("h p n -> p h n"), ocst.t[:], [ocst], [self.oc])
            S.barrier()


    def phase_merge(self, l):
        nc, S = self.nc, self.S
        last = (l == L - 1)
        with ExitStack() as es:
            wst = self.sb(es, "mw", [128, 8, 512], F32)
            Wbr = [self.sb(es, "mWbr%d" % r, [128, 4, 1024], BF16) for r in range(3)]
            Wo = self.sb(es, "mWo", [128, 8, 1024], BF16)
            for r in range(3):
                for hf in range(2):
                    S.dma("sp", wst.t[:, 0:4, :], self.w_br[r].t[l, :, hf * 512:(hf + 1) * 512].rearrange("(kc p) n -> p kc n", p=128), [], [wst])
                    S.op("pool", lambda e: e.tensor_copy(out=Wbr[r].t[:, :, hf * 512:(hf + 1) * 512], in_=wst.t[:, 0:4, :]), [wst], [Wbr[r]])
            for hf in range(2):
                S.dma("sp", wst.t[:], self.w_o.t[l, :, hf * 512:(hf + 1) * 512].rearrange("(kc p) n -> p kc n", p=128), [], [wst])
                S.op("pool", lambda e: e.tensor_copy(out=Wo.t[:, :, hf * 512:(hf + 1) * 512], in_=wst.t[:]), [wst], [Wo])
            ob_ = [[self.sb(es, "mo%d_%d" % (r, i), [128, 4, 512], BF16) for r in range(3)] for i in range(2)]
            gt = [self.sb(es, "mg%d" % i, [128, 24, 512], BF16) for i in range(2)]
            xt = [self.sb(es, "mx%d" % i, [128, 8, 512], F32) for i in range(2)]
            mrg = [self.sb(es, "mm%d" % i, [128, 8, 512], BF16) for i in range(2)]
            m1 = [self.sb(es, "m1_%d" % i, [128, 512], F32) for i in range(2)]
            m2 = [self.sb(es, "m2_%d" % i, [128, 512], F32) for i in range(2)]
            m3 = [self.sb(es, "m3_%d" % i, [128, 512], F32) for i in range(2)]
            srcs = (self.oa, self.ob, self.oc)
            xsv = self.xs.t.rearrange("(kc p) t -> p kc t", p=128)
            u = 0
            for ti, (t0, n_) in enumerate(TILES):
                if ti == 0 and last:
                    continue
                s = 1 if ti == 0 else 0
                o3, g, x, mg = ob_[ti % 2], gt[ti % 2], xt[ti % 2], mrg[ti % 2]
                for r in range(3):
                    S.dma("sp", o3[r].t[:, :, 0:n_], srcs[r].t[:, :, t0:t0 + n_].rearrange("h p n -> p h n"), [srcs[r]], [o3[r]])
                S.dma("sp", g.t[:, :, 0:n_], self.gat.t[:, :, t0:t0 + n_].rearrange("h p n -> p h n"), [self.gat], [g])
                S.dma("sp", x.t[:, :, 0:n_], xsv[:, :, t0:t0 + n_], [self.xs], [x])
                for j in range(8):
                    a1, a2, a3 = m1[j % 2], m2[j % 2], m3[j % 2]
                    pss = [self.psum[(3 * u + r) % 6] for r in range(3)]
                    u += 1
                    for r in range(3):
                        S.mm(pss[r].t[:, 0:n_], [(Wbr[r].t[:, kc, j * 128:(j + 1) * 128], o3[r].t[:, kc, 0:n_]) for kc in range(4)], [Wbr[r], o3[r]], [pss[r]])
                    for r, a in enumerate((a1, a2, a3)):
                        S.op("dve", lambda e, r=r, a=a: e.tensor_tensor(out=a.t[:, 0:n_], in0=pss[r].t[:, 0:n_], in1=g.t[:, 8 * r + j, 0:n_], op=ALU.mult), [pss[r], g], [a])
                    S.op("pool", lambda e: e.tensor_tensor(out=a1.t[:, 0:n_], in0=a1.t[:, 0:n_], in1=a2.t[:, 0:n_], op=ALU.add), [a1, a2], [a1])
                    S.op("pool", lambda e: e.tensor_tensor(out=mg.t[:, j, 0:n_], in0=a1.t[:, 0:n_], in1=a3.t[:, 0:n_], op=ALU.add), [a1, a3], [mg])
                for j in range(8):
                    ps = self.psum[6 + j % 2]
                    S.mm(ps.t[:, 0:n_], [(Wo.t[:, kc, j * 128:(j + 1) * 128], mg.t[:, kc, 0:n_]) for kc in range(8)], [Wo, mg], [ps])
                    S.op("dve", lambda e: e.scalar_tensor_tensor(out=x.t[:, j, 0:n_], in0=ps.t[:, 0:n_], scalar=self.MOD.t[:, l, 16 + j, s:s + 1], in1=x.t[:, j, 0:n_],
                                                                  op0=ALU.mult, op1=ALU.add), [ps, x, self.MOD], [x])
                S.dma("pool", xsv[:, :, t0:t0 + n_], x.t[:, :, 0:n_], [x], [self.xs])
            S.barrier()

    def phase_ffn1(self, l):
        nc, S = self.nc, self.S
        sm = self.small
        W = T + 4
        with ExitStack() as es:
            wst = [self.sb(es, "fw%d" % i, [128, 8, 256], F32) for i in range(2)]
            wbf = [self.sb(es, "fwb%d" % i, [128, 8, 256], BF16) for i in range(2)]
            RB = [self.sb(es, "fRB%d" % i, [128, W], F32) for i in range(2)]
            cvb = [self.sb(es, "fcv%d" % i, [128, W], F32) for i in range(2)]
            stg = [self.sb(es, "fst%d" % i, [128, 512], BF16) for i in range(3)]
            for rb in RB:
                S.op("pool", lambda e, rb=rb: e.memset(rb.t[:], 0.0), [], [rb])
            pieces = [(0, 0, 256)] + [(258 + 512 * i, 256 + 512 * i, 512) for i in range(8)]
            u = 0
            sidx = 0
            cw = sm["fcwT"].t
            n2 = W - 2
            for c in range(22):
                rb, cv = RB[c % 2], cvb[c % 2]
                ws, wb = wst[c % 2], wbf[c % 2]
                S.dma("sp", ws.t[:, :, 0:128], self.ffn_w1.t[l, :, c * 128:(c + 1) * 128].rearrange("(kc p) n -> p kc n", p=128), [], [ws])
                S.dma("sp", ws.t[:, :, 128:256], self.ffn_w1.t[l, :, DFF + c * 128:DFF + (c + 1) * 128].rearrange("(kc p) n -> p kc n", p=128), [], [ws])
                S.op("pool", lambda e: e.tensor_copy(out=wb.t[:], in_=ws.t[:]), [ws], [wb])
                for ti, (t0, n_) in enumerate(TILES):
                    pq = self.psum[u % 2]
                    u += 1
                    S.mm(pq.t[:, 0:n_], [(wb.t[:, kc, 0:128], self.hT.t[:, kc, t0:t0 + n_]) for kc in range(8)], [wb, self.hT], [pq])
                    pos = 1 + t0 if ti == 0 else 3 + t0
                    S.op("act", lambda e, pq=pq, pos=pos, n_=n_: e.activation(out=rb.t[:, pos:pos + n_], in_=pq.t[:, 0:n_], func=AF.Copy), [pq], [rb])
                S.op("dve", lambda e: e.tensor_scalar(out=cv.t[:, 0:n2], in0=rb.t[:, 0:n2], scalar1=cw[:, l, c, 0:1], scalar2=None, op0=ALU.mult), [rb, sm["fcwT"]], [cv])
                S.op("dve", lambda e: e.scalar_tensor_tensor(out=cv.t[:, 0:n2], in0=rb.t[:, 1:n2 + 1], scalar=cw[:, l, c, 1:2], in1=cv.t[:, 0:n2], op0=ALU.mult, op1=ALU.add), [rb, cv, sm["fcwT"]], [cv])
                S.op("dve", lambda e: e.scalar_tensor_tensor(out=cv.t[:, 0:n2], in0=rb.t[:, 2:n2 + 2], scalar=cw[:, l, c, 2:3], in1=cv.t[:, 0:n2], op0=ALU.mult, op1=ALU.add), [rb, cv, sm["fcwT"]], [cv])
                S.op("act", lambda e: e.activation(out=cv.t[:, 0:n2], in_=cv.t[:, 0:n2], func=AF.Silu, bias=sm["fcbT"].t[:, l, c:c + 1]), [cv, sm["fcbT"]], [cv])
                for ti, (t0, n_) in enumerate(TILES):
                    ci = pieces[ti][0]
                    pq = self.psum[2 + u % 2]
                    u += 1
                    so = stg[sidx % 3]
                    sidx += 1
                    S.mm(pq.t[:, 0:n_], [(wb.t[:, kc, 128:256], self.hT.t[:, kc, t0:t0 + n_]) for kc in range(8)], [wb, self.hT], [pq])
                    S.op("dve", lambda e, pq=pq, so=so, ci=ci, n_=n_: e.tensor_tensor(out=so.t[:, 0:n_], in0=pq.t[:, 0:n_], in1=cv.t[:, ci:ci + n_], op=ALU.mult), [pq, cv], [so])
                    S.dma("pool", self.gff.t[c, :, t0:t0 + n_], so.t[:, 0:n_], [so], [self.gff])
            S.barrier()

    def phase_ffn2(self, l):
        nc, S = self.nc, self.S
        last = (l == L - 1)
        with ExitStack() as es:
            wst = self.sb(es, "f2w", [128, 22, 256], F32)
            W2 = self.sb(es, "fW2", [128, 22, 1024], BF16)
            for q4 in range(4):
                S.dma("sp", wst.t[:], self.ffn_w2.t[l, :, q4 * 256:(q4 + 1) * 256].rearrange("(kc p) n -> p kc n", p=128), [], [wst])
                S.op("pool", lambda e: e.tensor_copy(out=W2.t[:, :, q4 * 256:(q4 + 1) * 256], in_=wst.t[:]), [wst], [W2])
            gt = [self.sb(es, "f2g%d" % i, [128, 22, 512], BF16) for i in range(2)]
            xt = [self.sb(es, "f2x%d" % i, [128, 8, 512], F32) for i in range(2)]
            xsv = self.xs.t.rearrange("(kc p) t -> p kc t", p=128)
            yv = self.yT.t[self.bi].rearrange("(kc p) t -> p kc t", p=128)
            for ti, (t0, n_) in enumerate(TILES):
                if ti == 0 and last:
                    continue
                s = 1 if ti == 0 else 0
                g, x = gt[ti % 2], xt[ti % 2]
                S.dma("sp", g.t[:, :, 0:n_], self.gff.t[:, :, t0:t0 + n_].rearrange("h p n -> p h n"), [self.gff], [g])
                S.dma("sp", x.t[:, :, 0:n_], xsv[:, :, t0:t0 + n_], [self.xs], [x])
                for j in range(8):
                    ps = self.psum[j % 4]
                    S.mm(ps.t[:, 0:n_], [(W2.t[:, c, j * 128:(j + 1) * 128], g.t[:, c, 0:n_]) for c in range(22)], [W2, g], [ps])
                    S.op("dve", lambda e: e.scalar_tensor_tensor(out=x.t[:, j, 0:n_], in0=ps.t[:, 0:n_], scalar=self.MOD.t[:, l, 40 + j, s:s + 1], in1=x.t[:, j, 0:n_],
                                                                  op0=ALU.mult, op1=ALU.add), [ps, x, self.MOD], [x])
                if last:
                    S.dma("pool", yv[:, :, t0 - NCTX:t0 - NCTX + n_], x.t[:, :, 0:n_], [x], [self.yT])
                else:
                    S.dma("pool", xsv[:, :, t0:t0 + n_], x.t[:, :, 0:n_], [x], [self.xs])
            S.barrier()


def host_consts():
    c = np.zeros((128, 16, 128), np.float32)
    idx = np.arange(128)
    c[:, 0, :] = np.eye(128, dtype=np.float32)
    c[:, 1, :] = 1.0
    blk = (idx[:, None] // 64) == (idx[None, :] // 64)
    c[:, 2, :] = blk
    R = np.zeros((128, 128), np.float32)
    for m in range(128):
        if m % 32 < 16:
            R[m + 16, m] = -1.0
        else:
            R[m - 16, m] = 1.0
    c[:, 3, :] = R
    t_, c_ = idx[:, None], idx[None, :]
    c[:, 4, :] = blk & (t_ <= c_)
    c[:, 5, :] = blk & (t_ >= c_)
    c[:, 6, :] = blk & (t_ > c_)
    c[:, 7, :] = blk & (t_ < c_)
    c[:, 8, :] = -1.0
    c[:, 9, :] = np.where(blk & (c_ >= t_), 0.0, -30000.0)
    c[:, 10, :] = np.where(blk & (c_ <= t_), 0.0, -30000.0)
    c[:, 11, :] = blk & (c_ > t_)
    c[:, 12, :] = blk & (c_ < t_)
    return c


def rope_tables():
    rows = NLAT // 64
    row = np.repeat(np.arange(rows, dtype=np.int32), 64).astype(np.float32)
    col = np.tile(np.arange(64, dtype=np.int32), rows).astype(np.float32)
    d_axis = 32
    inv_freq = (np.float32(10000.0) ** (-np.arange(0, d_axis, 2, dtype=np.float32) / np.float32(d_axis))).astype(np.float32)
    ang_r = (row[:, None] * inv_freq[None, :]).astype(np.float32)
    ang_c = (col[:, None] * inv_freq[None, :]).astype(np.float32)
    cosT = np.ones((128, T), np.float32)
    sinT = np.zeros((128, T), np.float32)
    for p in range(128):
        d = p % 64
        ang = ang_r if d < 32 else ang_c
        f = d % 16
        cosT[p, NCTX:] = np.cos(ang[:, f])
        sinT[p, NCTX:] = np.sin(ang[:, f])
    return cosT, sinT


def pvec(v, chunks):
    return np.ascontiguousarray(np.asarray(v, np.float32).reshape(chunks, 128).T)


def make_in_maps(inp, nb=4):
    f = lambda a: np.ascontiguousarray(np.asarray(a, dtype=np.float32))
    cosT, sinT = rope_tables()
    shared = {
        "w_mod": f(inp["w_mod"]), "w_in": f(inp["w_in"]),
        "bmodT": np.ascontiguousarray(np.stack([pvec(inp["b_mod"][l], 48) for l in range(L)], axis=1)),
        "n1g": np.ascontiguousarray(np.stack([pvec(inp["norm1_g"][l], 8) for l in range(L)], axis=1)),
        "n2g": np.ascontiguousarray(np.stack([pvec(inp["norm2_g"][l], 8) for l in range(L)], axis=1)),
        "qkg": np.ascontiguousarray(np.stack([np.stack([np.tile(f(inp[k][l]), 2) for k in ("da_qn_g", "da_kn_g", "gq_qn_g", "gq_kn_g")], axis=1)
                                              for l in range(L)], axis=1)),
        "ropec": cosT, "ropes": sinT,
        "lamT": np.ascontiguousarray(np.transpose(f(inp["da_lambda"]), (2, 0, 1))),
        "sublnT": np.ascontiguousarray(f(inp["da_subln_g"]).T),
        "dnconvT": np.ascontiguousarray(np.transpose(f(inp["dn_conv_w"]).reshape(L, 3, 12, 128), (3, 0, 2, 1))),
        "dnalog": np.ascontiguousarray(np.broadcast_to(f(inp["dn_a_log"]).reshape(1, L, 8), (128, L, 8))),
        "dndtb": np.ascontiguousarray(np.broadcast_to(f(inp["dn_dt_bias"]).reshape(1, L, 8), (128, L, 8))),
        "dnng": np.ascontiguousarray(f(inp["dn_norm_g"]).T),
        "w_br_a": f(inp["w_br_a"]), "w_br_b": f(inp["w_br_b"]), "w_br_c": f(inp["w_br_c"]),
        "w_o": f(inp["w_o"]), "ffn_w1": f(inp["ffn_w1"]),
        "fcwT": np.ascontiguousarray(np.transpose(f(inp["ffn_conv_w"]).reshape(L, 3, 22, 128), (3, 0, 2, 1))),
        "fcbT": np.ascontiguousarray(np.transpose(f(inp["ffn_conv_b"]).reshape(L, 22, 128), (2, 0, 1))),
        "ffn_w2": f(inp["ffn_w2"]),
        "cst": host_consts(),
    }
    x, ctx, c, c_ctx = f(inp["x"]), f(inp["ctx"]), f(inp["c"]), f(inp["c_ctx"])
    m = dict(shared)
    m["xT"] = np.ascontiguousarray(np.stack([np.concatenate([ctx[b], x[b]], axis=0).T for b in range(nb)], axis=0))
    m["cT"] = np.ascontiguousarray(np.stack([np.stack([pvec(c[b], 8), pvec(c_ctx, 8)], axis=2) for b in range(nb)], axis=1))
    return [m]


_PROG = {}


def kernel(**inp):
    if "p" not in _PROG:
        p = Prog()
        p.build()
        _PROG["p"] = p
    p = _PROG["p"]
    maps = make_in_maps(inp, 4)
    res = run_bass_kernel_spmd(p.nc, maps, core_ids=[0])
    y = res.results[0]["yT"]
    out = np.stack([np.ascontiguousarray(y[b].T) for b in range(4)], axis=0)
    return out.astype(np.float32)
```

```python
import math
from contextlib import ExitStack
import numpy as np
import concourse.bass as bass
import concourse.mybir as mybir
from concourse.bass_utils import run_bass_kernel_spmd

F32 = mybir.dt.float32
BF16 = mybir.dt.bfloat16
ALU = mybir.AluOpType
AF = mybir.ActivationFunctionType
AX = mybir.AxisListType

D = 1024
L = 4
NCTX = 256
NLAT = 4096
T = NCTX + NLAT
NIN = 7440
DFF = 2816
EPS = 1e-6
NO_SWDGE = True
TILES = [(0, 256)] + [(256 + 512 * i, 512) for i in range(8)]
NT128 = T // 128


class Buf:
    __slots__ = ("t", "w", "r", "dsem", "dcnt", "name")

    def __init__(self, t, name=""):
        self.t = t
        self.w = None
        self.r = []
        self.dsem = None
        self.dcnt = 0
        self.name = name

    def __getitem__(self, idx):
        return self.t[idx]


class Sched:
    def __init__(self, nc):
        self.nc = nc
        self.engs = {"pe": nc.tensor, "act": nc.scalar, "dve": nc.vector,
                     "pool": nc.gpsimd, "sp": nc.sync}
        self.sem = {}
        self.cnt = {}
        self.seen = {k: {} for k in self.engs}
        for k in self.engs:
            self.sem[k] = nc.alloc_semaphore(name="prog_" + k)
            self.cnt[k] = 0
        self.dbufs = []
        self.freesems = []
        self.ninst = 0
        self.nwaits = 0

    def _resolve(self, tok):
        if tok[0] == "e":
            return self.sem[tok[1]], tok[2], tok[1]
        b = tok[1]
        return b.dsem, 16 * b.dcnt, id(b.dsem)

    def _wait(self, en, toks):
        eng = self.engs[en]
        need = {}
        for tok in toks:
            if tok is None:
                continue
            if en == "pe" and tok[0] == "e" and tok[1] == "pe":
                continue
            if tok[0] == "d" and tok[1].dsem is None:
                continue
            sem, val, key = self._resolve(tok)
            if self.seen[en].get(key, 0) >= val:
                continue
            if key not in need or need[key][1] < val:
                need[key] = (sem, val)
        for key, (sem, val) in need.items():
            eng.wait_ge(sem, val)
            self.seen[en][key] = val
            self.nwaits += 1

    def _deps(self, reads, writes, nowaw=False):
        toks = []
        for b in reads:
            toks.append(b.w)
        for b in writes:
            if not nowaw:
                toks.append(b.w)
            toks.extend(b.r)
        return toks

    def _compact(self, toks):
        best = {}
        for t in toks:
            if t[0] == "e":
                k = t[1]
                if k not in best or best[k][2] < t[2]:
                    best[k] = t
            else:
                best[id(t[1])] = t
        return list(best.values())

    def _commit(self, tok, reads, writes):
        for b in reads:
            if b not in writes:
                b.r.append(tok)
                if len(b.r) > 16:
                    b.r = self._compact(b.r)
        for b in writes:
            b.w = tok
            b.r = []

    def op(self, en, fn, reads=(), writes=()):
        self._wait(en, self._deps(reads, writes))
        ins = fn(self.engs[en])
        self.cnt[en] += 1
        ins.then_inc(self.sem[en], 1)
        self._commit(("e", en, self.cnt[en]), reads, writes)
        self.ninst += 1
        return ins

    def mm(self, out_ap, pairs, reads=(), writes=(), **kw):
        self._wait("pe", self._deps(reads, writes))
        n = len(pairs)
        ins = None
        for i, (l, r) in enumerate(pairs):
            ins = self.nc.tensor.matmul(out_ap, l, r, start=(i == 0), stop=(i == n - 1), **kw)
        self.cnt["pe"] += 1
        ins.then_inc(self.sem["pe"], 1)
        self._commit(("e", "pe", self.cnt["pe"]), reads, writes)
        self.ninst += n
        return ins

    def mm1(self, out_ap, l, r, start, stop, reads=(), writes=(), **kw):
        self._wait("pe", self._deps(reads, writes))
        ins = self.nc.tensor.matmul(out_ap, l, r, start=start, stop=stop, **kw)
        self.cnt["pe"] += 1
        ins.then_inc(self.sem["pe"], 1)
        self._commit(("e", "pe", self.cnt["pe"]), reads, writes)
        self.ninst += 1
        return ins

    def transpose(self, out_ap, in_ap, ident_ap, reads=(), writes=()):
        self._wait("pe", self._deps(reads, writes))
        ins = self.nc.tensor.transpose(out_ap, in_ap, ident_ap)
        self.cnt["pe"] += 1
        ins.then_inc(self.sem["pe"], 1)
        self._commit(("e", "pe", self.cnt["pe"]), reads, writes)
        self.ninst += 1
        return ins

    def dma(self, q, out_ap, in_ap, reads=(), writes=(), **kw):
        if NO_SWDGE and q == "pool":
            q = "sp"
        assert len(writes) == 1
        dst = writes[0]
        toks = self._deps(reads, writes)
        if dst.w is not None and dst.w[0] == "d" and dst.w[1] is dst:
            toks = [t for t in toks if t is not dst.w]
        self._wait(q, toks)
        if dst.dsem is None:
            if self.freesems:
                dst.dsem, dst.dcnt = self.freesems.pop()
            else:
                self.nsem = getattr(self, "nsem", 0) + 1
                dst.dsem, dst.dcnt = self.nc.alloc_semaphore(name="dsem%d" % self.nsem), 0
            self.dbufs.append(dst)
        ins = self.engs[q].dma_start(out=out_ap, in_=in_ap, **kw)
        dst.dcnt += 1
        ins.then_inc(dst.dsem, 16)
        tok = ("d", dst)
        for b in reads:
            b.r.append(tok)
            if len(b.r) > 16:
                b.r = self._compact(b.r)
        dst.w = tok
        dst.r = []
        self.ninst += 1
        return ins

    def release_buf(self, b):
        if b.dsem is not None:
            self.freesems.append((b.dsem, b.dcnt))
            self.dbufs.remove(b)
            b.dsem = None

    def barrier(self):
        toks = [("e", k, self.cnt[k]) for k in self.engs if self.cnt[k] > 0]
        toks += [("d", b) for b in self.dbufs]
        for en in self.engs:
            self._wait(en, toks)

    def finish(self, bufs):
        self.barrier()


def lam_init_of(l):
    return 0.8 - 0.6 * math.exp(-0.3 * l)


class Prog:
    def __init__(self, nlayers=L, dbg=None):
        self.nl = nlayers
        self.dbg = dbg or set()
        nc = bass.Bass("TRN2", target_bir_lowering=False)
        self.nc = nc
        self.S = Sched(nc)
        self.inputs = {}
        self.outs = {}
        self.psum = [Buf(nc.alloc_psum_tensor("ps%d" % i, [128, 512], F32), "ps%d" % i) for i in range(8)]
        self.wq = 0

    def din(self, name, shape, dt=F32):
        t = self.nc.dram_tensor(name, list(shape), dt, kind="ExternalInput")
        b = Buf(t.ap(), name)
        self.inputs[name] = b
        return b

    def dscratch(self, name, shape, dt):
        kind = "ExternalOutput" if (name in self.dbg or name == "yT") else "Internal"
        t = self.nc.dram_tensor(name, list(shape), dt, kind=kind)
        b = Buf(t.ap(), name)
        if kind == "ExternalOutput":
            self.outs[name] = b
        return b

    def sb(self, es, name, shape, dt):
        self.uid = getattr(self, "uid", 0) + 1
        name = "%s_%d" % (name, self.uid)
        t = es.enter_context(self.nc.sbuf_tensor(name, list(shape), dt))
        b = Buf(t, name)
        es.callback(self.S.release_buf, b)
        return b

    def dmaq(self):
        self.wq += 1
        return "sp" if self.wq % 2 else "pool"
        return "sp" if self.wq % 2 else "pool"

    def declare(self):
        nl = self.nl
        di = self.din
        self.xT = di("xT", [D, T])
        self.cT = di("cT", [128, 8, 2])
        self.w_mod = di("w_mod", [L, D, 6 * D])
        self.bmodT = di("bmodT", [128, L, 48])
        self.n1g = di("n1g", [128, L, 8])
        self.n2g = di("n2g", [128, L, 8])
        self.w_in = di("w_in", [L, D, NIN])
        self.qkg = di("qkg", [128, L, 4])
        self.ropec = di("ropec", [128, T])
        self.ropes = di("ropes", [128, T])
        self.lamT = di("lamT", [64, L, 4])
        self.sublnT = di("sublnT", [128, L])
        self.dnconvT = di("dnconvT", [128, L, 12, 3])
        self.dnalog = di("dnalog", [128, L, 8])
        self.dndtb = di("dndtb", [128, L, 8])
        self.dnng = di("dnng", [128, L])
        self.w_br = [di("w_br_a", [L, 512, D]), di("w_br_b", [L, 512, D]), di("w_br_c", [L, 512, D])]
        self.w_o = di("w_o", [L, D, D])
        self.ffn_w1 = di("ffn_w1", [L, D, 2 * DFF])
        self.fcwT = di("fcwT", [128, L, 22, 3])
        self.fcbT = di("fcbT", [128, L, 22])
        self.ffn_w2 = di("ffn_w2", [L, DFF, D])
        self.cst = di("cst", [128, 16, 128])
        ds = self.dscratch
        self.xs = ds("xs", [D, T], F32)
        self.daq = ds("daq", [4, 128, T], BF16)
        self.dak = ds("dak", [4, 128, T], BF16)
        self.dav = ds("dav", [T, 512], BF16)
        self.gqq = ds("gqq", [4, 128, T], BF16)
        self.gqk = ds("gqk", [2, 128, T], BF16)
        self.gqv = ds("gqv", [T, 128], BF16)
        self.dnq = ds("dnq", [4, 128, T], BF16)
        self.dnk = ds("dnk", [4, 128, T], BF16)
        self.dnv = ds("dnv", [4, 128, T], BF16)
        self.dng = ds("dng", [T, 8], F32)
        self.dnb = ds("dnb", [T, 8], F32)
        self.dnz = ds("dnz", [4, 128, T], BF16)
        self.gat = ds("gat", [24, 128, T], BF16)
        self.oa = ds("oa", [4, 128, T], BF16)
        self.ob = ds("ob", [4, 128, T], BF16)
        self.of = ds("of", [4, 128, T], F32)
        self.oc = ds("oc", [4, 128, T], BF16)
        self.gff = ds("gff", [22, 128, T], BF16)
        self.yT = ds("yT", [D, NLAT], F32)

    def build(self, stop=None):
        nc, S = self.nc, self.S
        self.declare()
        with ExitStack() as es0:
            self.C = self.sb(es0, "C", [128, 16, 128], F32)
            self.Cb = self.sb(es0, "Cb", [128, 16, 128], BF16)
            self.MOD = self.sb(es0, "MOD", [128, L, 48, 2], F32)
            self.GS = self.sb(es0, "GS", [128, L, 2, 8, 2], F32)
            self.small = {}
            for nm, src, shp in (("n1g", self.n1g, [128, L, 8]), ("n2g", self.n2g, [128, L, 8]),
                                 ("bmodT", self.bmodT, [128, L, 48]), ("qkg", self.qkg, [128, L, 4]),
                                 ("sublnT", self.sublnT, [128, L]), ("dnconvT", self.dnconvT, [128, L, 12, 3]),
                                 ("dnalog", self.dnalog, [128, L, 8]), ("dndtb", self.dndtb, [128, L, 8]),
                                 ("dnng", self.dnng, [128, L]), ("fcwT", self.fcwT, [128, L, 22, 3]),
                                 ("fcbT", self.fcbT, [128, L, 22]), ("cT", self.cT, [128, 8, 2])):
                b = self.sb(es0, "s_" + nm, shp, F32)
                S.dma(self.dmaq(), b.t[:], src.t, [src], [b])
                self.small[nm] = b
            self.lam = self.sb(es0, "s_lam", [64, L, 4], F32)
            S.dma("sp", self.lam.t[:], self.lamT.t, [self.lamT], [self.lam])
            self.NLAM = self.sb(es0, "NLAM", [128, L], F32)
            S.dma("sp", self.C.t[:], self.cst.t, [self.cst], [self.C])
            S.op("dve", lambda e: e.tensor_copy(out=self.Cb.t[:], in_=self.C.t[:]), [self.C], [self.Cb])
            self.phase_mods()
            if stop == "mods":
                return self.end(es0)
            S.dma("sp", self.xs.t, self.xT.t, [self.xT], [self.xs])
            for l in range(self.nl):
                last = (l == L - 1)
                with ExitStack() as esh:
                    self.hT = self.sb(esh, "hT", [128, 8, T + 8], BF16)
                    self.phase_norm(l, 0)
                    self.phase_proj_a(l)
                    self.phase_proj_dn(l)
                if stop == "proj":
                    return self.end(es0)
                self.phase_attn(l)
                if stop == "attn":
                    return self.end(es0)
                self.phase_delta(l)
                if stop == "delta":
                    return self.end(es0)
                self.phase_merge(l)
                if stop == "merge":
                    return self.end(es0)
                with ExitStack() as esh:
                    self.hT = self.sb(esh, "hT", [128, 8, T + 8], BF16)
                    self.phase_norm(l, 1)
                    self.phase_ffn1(l)
                self.phase_ffn2(l)
            self.end(es0)

    def end(self, es0):
        self.S.barrier()

    def cm(self, i, bf=False):
        return (self.Cb if bf else self.C).t[:, i, :]

    def phase_mods(self):
        nc, S = self.nc, self.S
        sm = self.small
        with ExitStack() as es:
            sc = self.sb(es, "sc", [128, 8, 2], F32)
            S.op("act", lambda e: e.activation(out=sc.t[:], in_=sm["cT"].t[:], func=AF.Silu), [sm["cT"]], [sc])
            wst = [self.sb(es, "wm%d" % i, [128, 8, 512], F32) for i in range(2)]
            k = 0
            for l in range(self.nl):
                for g in range(12):
                    w = wst[k % 2]
                    k += 1
                    src = self.w_mod.t[l, :, g * 512:(g + 1) * 512].rearrange("(kc p) n -> p kc n", p=128)
                    S.dma(self.dmaq(), w.t[:], src, [self.w_mod], [w])
                    for s4 in range(4):
                        j = g * 4 + s4
                        ps = self.psum[j % 2]
                        S.mm(ps.t[:, 0:2], [(w.t[:, kc, s4 * 128:(s4 + 1) * 128], sc.t[:, kc, :]) for kc in range(8)],
                             [w, sc], [ps])
                        S.op("dve", lambda e, ps=ps, l=l, j=j: e.tensor_scalar(
                            out=self.MOD.t[:, l, j, :], in0=ps.t[:, 0:2], scalar1=sm["bmodT"].t[:, l, j:j + 1],
                            scalar2=None, op0=ALU.add), [ps, sm["bmodT"]], [self.MOD])
            for l in range(self.nl):
                for n, (gname, j0) in enumerate((("n1g", 8), ("n2g", 32))):
                    for s in range(2):
                        S.op("dve", lambda e, l=l, n=n, s=s, gname=gname, j0=j0: e.scalar_tensor_tensor(
                            out=self.GS.t[:, l, n, :, s], in0=self.MOD.t[:, l, j0:j0 + 8, s], scalar=1.0,
                            in1=sm[gname].t[:, l, :], op0=ALU.add, op1=ALU.mult), [self.MOD, sm[gname]], [self.GS])
            pr = self.sb(es, "lampr", [64, L, 2], F32)
            S.op("dve", lambda e: e.tensor_tensor(out=pr.t[:, :, 0], in0=self.lam.t[:, :, 0], in1=self.lam.t[:, :, 1], op=ALU.mult), [self.lam], [pr])
            S.op("dve", lambda e: e.tensor_tensor(out=pr.t[:, :, 1], in0=self.lam.t[:, :, 2], in1=self.lam.t[:, :, 3], op=ALU.mult), [self.lam], [pr])
            ps = self.psum[2]
            S.mm(ps.t[:, 0:2 * L], [(self.C.t[0:64, 1, :], pr.t[:].rearrange("p l s -> p (l s)"))], [self.C, pr], [ps])
            ex = self.sb(es, "lamex", [128, L, 2], F32)
            S.op("act", lambda e: e.activation(out=ex.t[:].rearrange("p l s -> p (l s)"), in_=ps.t[:, 0:2 * L], func=AF.Exp), [ps], [ex])
            for l in range(L):
                S.op("dve", lambda e, l=l: e.scalar_tensor_tensor(
                    out=self.NLAM.t[:, l:l + 1], in0=ex.t[:, l, 1:2], scalar=-lam_init_of(l), in1=ex.t[:, l, 0:1],
                    op0=ALU.add, op1=ALU.subtract), [ex], [self.NLAM])
            S.barrier()

    def phase_norm(self, l, n):
        nc, S = self.nc, self.S
        jshift = 0 if n == 0 else 24
        with ExitStack() as es:
            xt = [self.sb(es, "nx%d" % i, [128, 8, 512], F32) for i in range(2)]
            sq = [self.sb(es, "nsq%d" % i, [128, 8, 512], BF16) for i in range(2)]
            rs = [self.sb(es, "nrs%d" % i, [128, 512], F32) for i in range(2)]
            tmp = [self.sb(es, "ntmp%d" % i, [128, 512], F32) for i in range(3)]
            k = 0
            xsv = self.xs.t.rearrange("(kc p) t -> p kc t", p=128)
            for ti, (t0, n_) in enumerate(TILES):
                s = 1 if ti == 0 else 0
                x, q, r = xt[ti % 2], sq[ti % 2], rs[ti % 2]
                S.dma(self.dmaq(), x.t[:, :, 0:n_], xsv[:, :, t0:t0 + n_], [self.xs], [x])
                S.op("act", lambda e, x=x, q=q, n_=n_: e.activation(out=q.t[:, :, 0:n_], in_=x.t[:, :, 0:n_], func=AF.Square), [x], [q])
                ps = self.psum[ti % 2]
                S.mm(ps.t[:, 0:n_], [(self.cm(1, True), q.t[:, kc, 0:n_]) for kc in range(8)], [self.Cb, q], [ps])
                S.op("act", lambda e, ps=ps, r=r, n_=n_: e.activation(out=r.t[:, 0:n_], in_=ps.t[:, 0:n_], func=AF.Sqrt, scale=1.0 / D, bias=EPS), [ps], [r])
                S.op("dve", lambda e, r=r, n_=n_: e.reciprocal(out=r.t[:, 0:n_], in_=r.t[:, 0:n_]), [r], [r])
                for kc in range(8):
                    tm = tmp[k % 3]
                    k += 1
                    S.op("dve", lambda e, tm=tm, x=x, r=r, kc=kc, n_=n_, s=s: e.scalar_tensor_tensor(
                        out=tm.t[:, 0:n_], in0=x.t[:, kc, 0:n_], scalar=self.GS.t[:, l, n, kc, s:s + 1], in1=r.t[:, 0:n_],
                        op0=ALU.mult, op1=ALU.mult), [x, r, self.GS], [tm])
                    S.op("act", lambda e, tm=tm, kc=kc, n_=n_, t0=t0, s=s: e.activation(
                        out=self.hT.t[:, kc, t0:t0 + n_], in_=tm.t[:, 0:n_], func=AF.Identity,
                        bias=self.MOD.t[:, l, jshift + kc, s:s + 1]), [tm, self.MOD], [self.hT])
            S.barrier()

    def load_w(self, wst, wbf, src2d, ncols, kch=8, dup64=False):
        S = self.S
        S.dma("sp", wst.t[:, 0:kch, 0:ncols], src2d.rearrange("(kc p) n -> p kc n", p=128), [], [wst])
        if dup64:
            for i, (d0, s0) in enumerate(((0, 0), (64, 0), (128, 64), (192, 64))):
                S.op("pool", lambda e, d0=d0, s0=s0: e.tensor_copy(out=wbf.t[:, 0:kch, d0:d0 + 64], in_=wst.t[:, 0:kch, s0:s0 + 64]), [wst], [wbf])
        else:
            S.op("pool", lambda e: e.tensor_copy(out=wbf.t[:, 0:kch, 0:ncols], in_=wst.t[:, 0:kch, 0:ncols]), [wst], [wbf])

    def phase_proj_a(self, l):
        nc, S = self.nc, self.S
        sm = self.small
        with ExitStack() as es:
            wst = [self.sb(es, "pw%d" % i, [128, 8, 512], F32) for i in range(2)]
            wbf = [self.sb(es, "pwb%d" % i, [128, 8, 512], BF16) for i in range(2)]
            cosT = self.sb(es, "cosT", [128, T], F32)
            sinT = self.sb(es, "sinT", [128, T], F32)
            S.dma("sp", cosT.t[:], self.ropec.t, [self.ropec], [cosT])
            S.dma("sp", sinT.t[:], self.ropes.t, [self.ropes], [sinT])
            sqb = [self.sb(es, "psq%d" % i, [128, 512], BF16) for i in range(2)]
            rsb = [self.sb(es, "prs%d" % i, [128, 512], F32) for i in range(2)]
            xnb = [self.sb(es, "pxn%d" % i, [128, 512], F32) for i in range(2)]
            t1b = [self.sb(es, "pt1%d" % i, [128, 512], F32) for i in range(2)]
            t2b = [self.sb(es, "pt2%d" % i, [128, 512], F32) for i in range(2)]
            stg = [self.sb(es, "pst%d" % i, [128, 512], BF16) for i in range(3)]
            stf = [self.sb(es, "psf%d" % i, [128, 16], F32) for i in range(3)]
            nea = self.sb(es, "nea", [128, 8], F32)
            S.op("act", lambda e: e.activation(out=nea.t[:], in_=sm["dnalog"].t[:, l, :], func=AF.Exp), [sm["dnalog"]], [nea])
            st = {"w": 0, "u": 0, "s": 0}
            win = self.w_in.t

            def getw(col0, ncols, dup=False):
                i = st["w"] % 2
                st["w"] += 1
                self.load_w(wst[i], wbf[i], win[l, :, col0:col0 + ncols], ncols, dup64=dup)
                return wbf[i]

            def store(dst_buf, dst_ap, src_buf, src_ap):
                S.dma("pool", dst_ap, src_ap, [src_buf], [dst_buf])

            def fm(wb, nch, post):
                for j in range(nch):
                    for ti, (t0, n_) in enumerate(TILES):
                        u = st["u"]
                        st["u"] += 1
                        pq = self.psum[u % 2]
                        S.mm(pq.t[:, 0:n_], [(wb.t[:, kc, j * 128:(j + 1) * 128], self.hT.t[:, kc, t0:t0 + n_]) for kc in range(8)],
                             [wb, self.hT], [pq])
                        post(j, u, pq, t0, n_)

            def post_rope(dst, gi):
                def f(j, u, pq, t0, n_):
                    sq, rs, xn, t1, t2 = sqb[u % 2], rsb[u % 2], xnb[u % 2], t1b[u % 2], t2b[u % 2]
                    ps2, ps3 = self.psum[2 + u % 2], self.psum[4 + u % 2]
                    so = stg[st["s"] % 3]
                    st["s"] += 1
                    S.op("act", lambda e: e.activation(out=sq.t[:, 0:n_], in_=pq.t[:, 0:n_], func=AF.Square), [pq], [sq])
                    S.mm(ps2.t[:, 0:n_], [(self.cm(2, True), sq.t[:, 0:n_])], [self.Cb, sq], [ps2])
                    S.op("act", lambda e: e.activation(out=rs.t[:, 0:n_], in_=ps2.t[:, 0:n_], func=AF.Sqrt, scale=1.0 / 64, bias=EPS), [ps2], [rs])
                    S.op("dve", lambda e: e.reciprocal(out=rs.t[:, 0:n_], in_=rs.t[:, 0:n_]), [rs], [rs])
                    S.op("dve", lambda e: e.scalar_tensor_tensor(out=xn.t[:, 0:n_], in0=pq.t[:, 0:n_], scalar=sm["qkg"].t[:, l, gi:gi + 1],
                                                                  in1=rs.t[:, 0:n_], op0=ALU.mult, op1=ALU.mult), [pq, rs, sm["qkg"]], [xn])
                    S.mm(ps3.t[:, 0:n_], [(self.cm(3), xn.t[:, 0:n_])], [self.C, xn], [ps3])
                    S.op("pool", lambda e: e.tensor_tensor(out=t1.t[:, 0:n_], in0=xn.t[:, 0:n_], in1=cosT.t[:, t0:t0 + n_], op=ALU.mult), [xn, cosT], [t1])
                    S.op("dve", lambda e: e.tensor_tensor(out=t2.t[:, 0:n_], in0=ps3.t[:, 0:n_], in1=sinT.t[:, t0:t0 + n_], op=ALU.mult), [ps3, sinT], [t2])
                    S.op("pool", lambda e: e.tensor_tensor(out=so.t[:, 0:n_], in0=t1.t[:, 0:n_], in1=t2.t[:, 0:n_], op=ALU.add), [t1, t2], [so])
                    store(dst, dst.t[j, :, t0:t0 + n_], so, so.t[:, 0:n_])
                return f

            def post_act(dst, func, j0=0):
                def f(j, u, pq, t0, n_):
                    so = stg[st["s"] % 3]
                    st["s"] += 1
                    S.op("act", lambda e: e.activation(out=so.t[:, 0:n_], in_=pq.t[:, 0:n_], func=func), [pq], [so])
                    store(dst, dst.t[j0 + j, :, t0:t0 + n_], so, so.t[:, 0:n_])
                return f

            def tm(wb, ncols, post):
                for tb in range(NT128):
                    u = st["u"]
                    st["u"] += 1
                    pq = self.psum[u % 2]
                    S.mm(pq.t[:, 0:ncols], [(self.hT.t[:, kc, tb * 128:(tb + 1) * 128], wb.t[:, kc, 0:ncols]) for kc in range(8)],
                         [wb, self.hT], [pq])
                    post(tb, u, pq)

            def post_copy(dst, ncols):
                def f(tb, u, pq):
                    so = stg[st["s"] % 3]
                    st["s"] += 1
                    eng = "dve" if u % 2 else "act"
                    if eng == "dve":
                        S.op("dve", lambda e: e.tensor_copy(out=so.t[:, 0:ncols], in_=pq.t[:, 0:ncols]), [pq], [so])
                    else:
                        S.op("act", lambda e: e.activation(out=so.t[:, 0:ncols], in_=pq.t[:, 0:ncols], func=AF.Copy), [pq], [so])
                    store(dst, dst.t[tb * 128:(tb + 1) * 128, :], so, so.t[:, 0:ncols])
                return f

            def post_ba(tb, u, pq):
                so = stf[st["s"] % 3]
                st["s"] += 1
                S.op("act", lambda e: e.activation(out=so.t[:, 0:8], in_=pq.t[:, 0:8], func=AF.Sigmoid), [pq], [so])
                S.op("dve", lambda e: e.tensor_tensor(out=so.t[:, 8:16], in0=pq.t[:, 8:16], in1=sm["dndtb"].t[:, l, :], op=ALU.add), [pq, sm["dndtb"]], [so])
                S.op("act", lambda e: e.activation(out=so.t[:, 8:16], in_=so.t[:, 8:16], func=AF.Exp), [so], [so])
                S.op("act", lambda e: e.activation(out=so.t[:, 8:16], in_=so.t[:, 8:16], func=AF.Ln, bias=1.0), [so], [so])
                S.op("dve", lambda e: e.scalar_tensor_tensor(out=so.t[:, 8:16], in0=so.t[:, 8:16], scalar=-1.0, in1=nea.t[:], op0=ALU.mult, op1=ALU.mult), [so, nea], [so])
                store(self.dnb, self.dnb.t[tb * 128:(tb + 1) * 128, :], so, so.t[:, 0:8])
                store(self.dng, self.dng.t[tb * 128:(tb + 1) * 128, :], so, so.t[:, 8:16])

            tasks = [("fm", 0, 4, post_rope(self.daq, 0), False), ("fm", 512, 4, post_rope(self.dak, 1), False),
                     ("tm", 1024, 512, post_copy(self.dav, 512)), ("fm", 1536, 4, post_rope(self.gqq, 2), False),
                     ("fm", 2048, 2, post_rope(self.gqk, 3), True), ("tm", 2176, 128, post_copy(self.gqv, 128)),
                     ("tm", 3840, 16, post_ba), ("fm", 3856, 4, post_act(self.dnz, AF.Silu), False)]
            for g in range(6):
                tasks.append(("fm", 4368 + 512 * g, 4, post_act(self.gat, AF.Sigmoid, 4 * g), False))

            def wof(tk):
                if tk[0] == "fm":
                    return getw(tk[1], 128 if tk[4] else tk[2] * 128, tk[4])
                return getw(tk[1], tk[2])
            wnext = wof(tasks[0])
            for i, tk in enumerate(tasks):
                wcur = wnext
                if i + 1 < len(tasks):
                    wnext = wof(tasks[i + 1])
                if tk[0] == "fm":
                    fm(wcur, tk[2], tk[3])
                else:
                    tm(wcur, tk[2], tk[3])
            S.barrier()

    def phase_proj_dn(self, l):
        nc, S = self.nc, self.S
        sm = self.small
        W = T + 4
        with ExitStack() as es:
            wst = [self.sb(es, "dw%d" % i, [128, 8, 128], F32) for i in range(2)]
            wbf = [self.sb(es, "dwb%d" % i, [128, 8, 128], BF16) for i in range(2)]
            RB = [self.sb(es, "RB%d" % i, [128, W], F32) for i in range(2)]
            cv = self.sb(es, "dcv", [128, W], F32)
            sqb = self.sb(es, "dsq", [128, W], BF16)
            rsb = [self.sb(es, "drs%d" % i, [128, 512], F32) for i in range(2)]
            stg = [self.sb(es, "dst%d" % i, [128, 512], BF16) for i in range(3)]
            for rb in RB:
                S.op("pool", lambda e, rb=rb: e.memset(rb.t[:], 0.0), [], [rb])
            pieces = [(0, 0, 256)] + [(258 + 512 * i, 256 + 512 * i, 512) for i in range(8)]
            u = 0
            sidx = 0
            for c in range(12):
                rb = RB[c % 2]
                self.load_w(wst[c % 2], wbf[c % 2], self.w_in.t[l, :, 2304 + c * 128:2304 + (c + 1) * 128], 128)
                wb = wbf[c % 2]
                for ti, (t0, n_) in enumerate(TILES):
                    pq = self.psum[u % 2]
                    u += 1
                    S.mm(pq.t[:, 0:n_], [(wb.t[:, kc, :], self.hT.t[:, kc, t0:t0 + n_]) for kc in range(8)], [wb, self.hT], [pq])
                    pos = 1 + t0 if ti == 0 else 3 + t0
                    if ti % 2:
                        S.op("dve", lambda e, pq=pq, pos=pos, n_=n_: e.tensor_copy(out=rb.t[:, pos:pos + n_], in_=pq.t[:, 0:n_]), [pq], [rb])
                    else:
                        S.op("act", lambda e, pq=pq, pos=pos, n_=n_: e.activation(out=rb.t[:, pos:pos + n_], in_=pq.t[:, 0:n_], func=AF.Copy), [pq], [rb])
                cw = sm["dnconvT"].t
                n2 = W - 2
                S.op("dve", lambda e: e.tensor_scalar(out=cv.t[:, 0:n2], in0=rb.t[:, 0:n2], scalar1=cw[:, l, c, 0:1], scalar2=None, op0=ALU.mult), [rb, sm["dnconvT"]], [cv])
                S.op("dve", lambda e: e.scalar_tensor_tensor(out=cv.t[:, 0:n2], in0=rb.t[:, 1:n2 + 1], scalar=cw[:, l, c, 1:2], in1=cv.t[:, 0:n2], op0=ALU.mult, op1=ALU.add), [rb, cv, sm["dnconvT"]], [cv])
                S.op("dve", lambda e: e.scalar_tensor_tensor(out=cv.t[:, 0:n2], in0=rb.t[:, 2:n2 + 2], scalar=cw[:, l, c, 2:3], in1=cv.t[:, 0:n2], op0=ALU.mult, op1=ALU.add), [rb, cv, sm["dnconvT"]], [cv])
                S.op("act", lambda e: e.activation(out=cv.t[:, 0:n2], in_=cv.t[:, 0:n2], func=AF.Silu), [cv], [cv])
                dst = (self.dnq, self.dnk, self.dnv)[c // 4]
                hh = c % 4
                if c < 8:
                    S.op("act", lambda e: e.activation(out=sqb.t[:, 0:n2], in_=cv.t[:, 0:n2], func=AF.Square), [cv], [sqb])
                for (ci, tk, n_) in pieces:
                    so = stg[sidx % 3]
                    sidx += 1
                    if c < 8:
                        ps2 = self.psum[2 + sidx % 2]
                        rs = rsb[sidx % 2]
                        S.mm(ps2.t[:, 0:n_], [(self.cm(1, True), sqb.t[:, ci:ci + n_])], [self.Cb, sqb], [ps2])
                        S.op("act", lambda e, rs=rs, ps2=ps2, n_=n_: e.activation(out=rs.t[:, 0:n_], in_=ps2.t[:, 0:n_], func=AF.Sqrt, bias=EPS), [ps2], [rs])
                        S.op("dve", lambda e, rs=rs, n_=n_: e.reciprocal(out=rs.t[:, 0:n_], in_=rs.t[:, 0:n_]), [rs], [rs])
                        S.op("pool", lambda e, rs=rs, so=so, ci=ci, n_=n_: e.tensor_tensor(out=so.t[:, 0:n_], in0=cv.t[:, ci:ci + n_], in1=rs.t[:, 0:n_], op=ALU.mult), [cv, rs], [so])
                    else:
                        S.op("pool", lambda e, so=so, ci=ci, n_=n_: e.tensor_copy(out=so.t[:, 0:n_], in_=cv.t[:, ci:ci + n_]), [cv], [so])
                    S.dma("pool", dst.t[hh, :, tk:tk + n_], so.t[:, 0:n_], [so], [dst])
            S.barrier()


    def phase_attn(self, l):
        nc, S = self.nc, self.S
        sm = self.small
        last = (l == L - 1)
        li = lam_init_of(l)
        with ExitStack() as es:
            Kda = self.sb(es, "Kda", [128, 4, T], BF16)
            Kgq = self.sb(es, "Kgq", [128, 2, T], BF16)
            Vda = self.sb(es, "Vda", [128, NT128, 512], BF16)
            Vgq = self.sb(es, "Vgq", [128, NT128, 128], BF16)
            for h in range(4):
                S.dma("sp", Kda.t[:, h, :], self.dak.t[h], [self.dak], [Kda])
            for h in range(2):
                S.dma("sp", Kgq.t[:, h, :], self.gqk.t[h], [self.gqk], [Kgq])
            S.dma("sp", Vda.t[:], self.dav.t.rearrange("(c p) n -> p c n", p=128), [self.dav], [Vda])
            S.dma("sp", Vgq.t[:], self.gqv.t.rearrange("(c p) n -> p c n", p=128), [self.gqv], [Vgq])
            Qda = [self.sb(es, "Qda%d" % i, [128, 4, 512], BF16) for i in range(2)]
            Qgq = [self.sb(es, "Qgq%d" % i, [128, 4, 512], BF16) for i in range(2)]
            PT = [self.sb(es, "PT%d" % i, [128, 512], BF16) for i in range(4)]
            ftmp = [self.sb(es, "af%d" % i, [128, 512], F32) for i in range(6)]
            sqb = self.sb(es, "asq", [128, 512], BF16)
            stg = [self.sb(es, "ast%d" % i, [128, 512], BF16) for i in range(3)]
            sctr = [0]
            pctr = [0]
            ones_b = self.cm(1, True)
            SB = self.psum[0:4]
            OB = self.psum[4:8]

            def run_unit(kT, qT, vsel, mrows, n_, nkc):
                def qk(kc):
                    res = []
                    for sub in range(2):
                        sbk = SB[sctr[0] % 4]
                        sctr[0] += 1
                        ka, kb_ = kT(sub, kc)
                        qa, qb_ = qT(sub)
                        S.mm(sbk.t[:, 0:n_], [(ka, qa)], [kb_, qb_], [sbk])
                        pt = PT[pctr[0] % 4]
                        pctr[0] += 1
                        S.op("act", lambda e, pt=pt, sbk=sbk: e.activation(out=pt.t[:, 0:n_], in_=sbk.t[:, 0:n_], func=AF.Exp, scale=0.125), [sbk], [pt])
                        res.append(pt)
                    return res

                def pv(kc, pts):
                    for sub in range(2):
                        pt = pts[sub]
                        va, vb_ = vsel(sub, kc)
                        p0 = 0 if mrows == 128 else 64 * sub
                        S.mm1(OB[2 * sub].t[p0:p0 + mrows, 0:n_], va, pt.t[:, 0:n_], kc == 0, kc == nkc - 1, [vb_, pt], [OB[2 * sub]])
                        S.mm1(OB[2 * sub + 1].t[p0:p0 + mrows, 0:n_], ones_b[:, 0:mrows], pt.t[:, 0:n_], kc == 0, kc == nkc - 1, [self.Cb, pt], [OB[2 * sub + 1]])
                prev = qk(0)
                for kc in range(1, nkc):
                    cur = qk(kc)
                    pv(kc - 1, prev)
                    prev = cur
                pv(nkc - 1, prev)

            for ti, (t0, n_) in enumerate(TILES):
                if ti == 0 and last:
                    continue
                nkc = 2 if ti == 0 else NT128
                qd, qg = Qda[ti % 2], Qgq[ti % 2]
                for h in range(4):
                    S.dma("sp", qd.t[:, h, 0:n_], self.daq.t[h, :, t0:t0 + n_], [self.daq], [qd])
                    S.dma("sp", qg.t[:, h, 0:n_], self.gqq.t[h, :, t0:t0 + n_], [self.gqq], [qg])
                for h in range(4):
                    run_unit(lambda sub, kc: (Kda.t[64 * sub:64 * sub + 64, h, kc * 128:(kc + 1) * 128], Kda),
                             lambda sub: (qd.t[64 * sub:64 * sub + 64, h, 0:n_], qd),
                             lambda sub, kc: (Vda.t[:, kc, h * 128:(h + 1) * 128], Vda), 128, n_, nkc)
                    r1, a1, r2, a2, dd, rs = ftmp
                    S.op("dve", lambda e: e.reciprocal(out=r1.t[:, 0:n_], in_=OB[1].t[:, 0:n_]), [OB[1]], [r1])
                    S.op("dve", lambda e: e.tensor_tensor(out=a1.t[:, 0:n_], in0=OB[0].t[:, 0:n_], in1=r1.t[:, 0:n_], op=ALU.mult), [OB[0], r1], [a1])
                    S.op("dve", lambda e: e.reciprocal(out=r2.t[:, 0:n_], in_=OB[3].t[:, 0:n_]), [OB[3]], [r2])
                    S.op("dve", lambda e: e.tensor_tensor(out=a2.t[:, 0:n_], in0=OB[2].t[:, 0:n_], in1=r2.t[:, 0:n_], op=ALU.mult), [OB[2], r2], [a2])
                    S.op("dve", lambda e: e.scalar_tensor_tensor(out=dd.t[:, 0:n_], in0=a2.t[:, 0:n_], scalar=self.NLAM.t[:, l:l + 1], in1=a1.t[:, 0:n_],
                                                                  op0=ALU.mult, op1=ALU.add), [a2, a1, self.NLAM], [dd])
                    S.op("act", lambda e: e.activation(out=sqb.t[:, 0:n_], in_=dd.t[:, 0:n_], func=AF.Square), [dd], [sqb])
                    pss = SB[sctr[0] % 4]
                    sctr[0] += 1
                    S.mm(pss.t[:, 0:n_], [(ones_b, sqb.t[:, 0:n_])], [self.Cb, sqb], [pss])
                    f = 1.0 / ((1.0 - li) ** 2)
                    S.op("act", lambda e: e.activation(out=rs.t[:, 0:n_], in_=pss.t[:, 0:n_], func=AF.Sqrt, scale=f / 128.0, bias=EPS * f), [pss], [rs])
                    S.op("dve", lambda e: e.reciprocal(out=rs.t[:, 0:n_], in_=rs.t[:, 0:n_]), [rs], [rs])
                    so = stg[(ti * 8 + h) % 3]
                    S.op("dve", lambda e: e.scalar_tensor_tensor(out=so.t[:, 0:n_], in0=dd.t[:, 0:n_], scalar=sm["sublnT"].t[:, l:l + 1], in1=rs.t[:, 0:n_],
                                                                  op0=ALU.mult, op1=ALU.mult), [dd, rs, sm["sublnT"]], [so])
                    S.dma("pool", self.oa.t[h, :, t0:t0 + n_], so.t[:, 0:n_], [so], [self.oa])
                for c in range(4):
                    kv = c // 2
                    run_unit(lambda sub, kc: (Kgq.t[64 * sub:64 * sub + 64, kv, kc * 128:(kc + 1) * 128], Kgq),
                             lambda sub: (qg.t[64 * sub:64 * sub + 64, c, 0:n_], qg),
                             lambda sub, kc: (Vgq.t[:, kc, kv * 64:(kv + 1) * 64], Vgq), 64, n_, nkc)
                    so = stg[(ti * 8 + 4 + c) % 3]
                    r1 = ftmp[0]
                    for sub in range(2):
                        p0 = 64 * sub
                        S.op("dve", lambda e: e.reciprocal(out=r1.t[p0:p0 + 64, 0:n_], in_=OB[2 * sub + 1].t[p0:p0 + 64, 0:n_]), [OB[2 * sub + 1]], [r1])
                        S.op("dve", lambda e: e.tensor_tensor(out=so.t[p0:p0 + 64, 0:n_], in0=OB[2 * sub].t[p0:p0 + 64, 0:n_], in1=r1.t[p0:p0 + 64, 0:n_], op=ALU.mult), [OB[2 * sub], r1], [so])
                    S.dma("pool", self.ob.t[c, :, t0:t0 + n_], so.t[:, 0:n_], [so], [self.ob])
            S.barrier()


    def phase_delta(self, l):
        nc, S = self.nc, self.S
        sm = self.small
        C = self.C
        SC = 128.0 ** -0.5
        H4 = [128, 4, 128]
        with ExitStack() as es:
            G = self.sb(es, "dG", [128, NT128, 8], F32)
            Bt = self.sb(es, "dB", [128, NT128, 8], F32)
            NB = self.sb(es, "dNB", [128, NT128, 8], F32)
            S.dma("sp", G.t[:], self.dng.t.rearrange("(c p) n -> p c n", p=128), [self.dng], [G])
            S.dma("sp", Bt.t[:], self.dnb.t.rearrange("(c p) n -> p c n", p=128), [self.dnb], [Bt])
            S.op("dve", lambda e: e.tensor_scalar(out=NB.t[:], in0=Bt.t[:], scalar1=-1.0, scalar2=None, op0=ALU.mult), [Bt], [NB])
            qB = [self.sb(es, "dq%d" % i, [128, 4, 512], BF16) for i in range(2)]
            kB = [self.sb(es, "dk%d" % i, [128, 4, 512], BF16) for i in range(2)]
            vB = [self.sb(es, "dv%d" % i, [128, 4, 512], BF16) for i in range(2)]
            def w(name, dt=F32, n=1):
                r = [self.sb(es, "%s%d" % (name, i), H4, dt) for i in range(n)]
                return r if n > 1 else r[0]
            EG = self.sb(es, "dEG", [128, 8], F32)
            EGL = self.sb(es, "dEGL", [128, 2, 4], F32)
            GL, decI, decS, Tacc, Vtok, Ktok, Kg, Ktail, ub, wT, qk, qd, egb, vnew = [w(n_) for n_ in
                ("dGL", "ddecI", "ddecS", "dTacc", "dVtok", "dKtok", "dKg", "dKtail", "dub", "dwT", "dqk", "dqd", "degb", "dvnew")]
            Pm = w("dP", F32, 2)
            Ptm = w("dPt", F32, 2)
            ofl, osum, rsn, onrm = w("dofl"), w("dosum"), w("drsn"), w("donrm")
            sqn = w("dsqn", BF16)
            zt = w("dzt", BF16)
            ocst = w("docst", BF16)
            ofst = w("dofst")
            Sst = self.sb(es, "dS", H4, F32)
            b0, b1, b2, b3, b4, b5, b6, b7 = self.psum

            def bc_f(ap2d, P=128):
                return ap2d.unsqueeze(2).to_broadcast([P, 4, 128])

            def bc_h(ap2d):
                return ap2d.unsqueeze(1).to_broadcast(H4)

            def v3(ps):
                return ps.t[:, :].rearrange("p (h n) -> p h n", h=4)

            def tile_local(d, tl, qb, kb_, vb, off):
                jd = 4 * d
                g4 = G.t[:, tl, jd:jd + 4]
                S.mm(b0.t[:, 0:4], [(C.t[:, 4 + d, :], g4)], [C, G], [b0])
                S.mm(b0.t[:, 4:8], [(C.t[:, 6 + d, :], g4)], [C, G], [b0])
                S.mm(b0.t[:, 8:12], [(C.t[0:64, 1, :], G.t[0:64, tl, jd:jd + 4])], [C, G], [b0])
                S.mm(b0.t[:, 12:16], [(C.t[64:128, 1, :], G.t[64:128, tl, jd:jd + 4])], [C, G], [b0])
                S.op("act", lambda e: e.activation(out=EG.t[:], in_=b0.t[:, 0:8], func=AF.Exp), [b0], [EG])
                S.op("act", lambda e: e.activation(out=EGL.t[:].rearrange("p c h -> p (c h)"), in_=b0.t[:, 8:16], func=AF.Exp), [b0], [EGL])
                S.op("pool", lambda e: e.tensor_tensor(out=GL.t[:], in0=bc_h(C.t[:, 4 + d, :]), in1=bc_f(g4), op=ALU.mult), [C, G], [GL])
                for h in range(4):
                    kt = kb_.t[:, h, off:off + 128]
                    hs = slice(h * 128, (h + 1) * 128)
                    S.mm(b1.t[:, hs], [(kt, kt)], [kb_], [b1])
                    S.mm(b2.t[:, hs], [(kt, qb.t[:, h, off:off + 128])], [kb_, qb], [b2])
                    S.mm(b3.t[:, hs], [(C.t[:, 1, :], GL.t[:, h, :]), (GL.t[:, h, :], C.t[:, 8, :]), (C.t[:, 0, :], C.t[:, 9 + d, :])], [C, GL], [b3])
                S.op("act", lambda e: e.activation(out=decI.t[:].rearrange("p h n -> p (h n)"), in_=b3.t[:, :], func=AF.Exp), [b3], [decI])
                S.op("pool", lambda e: e.tensor_tensor(out=decS.t[:], in0=decI.t[:], in1=bc_h(C.t[:, 11 + d, :]), op=ALU.mult), [decI, C], [decS])
                for h in range(4):
                    S.mm(b3.t[:, h * 128:(h + 1) * 128], [(C.t[:, 1, :], GL.t[:, h, :])], [C, GL], [b3])
                S.op("act", lambda e: e.activation(out=egb.t[:].rearrange("p h n -> p (h n)"), in_=b3.t[:, :], func=AF.Exp), [b3], [egb])
                pt, pp = Ptm[0], Pm[0]
                S.op("dve", lambda e: e.tensor_tensor(out=pt.t[:], in0=v3(b1), in1=decS.t[:], op=ALU.mult), [b1, decS], [pt])
                S.op("pool", lambda e: e.tensor_tensor(out=pt.t[:], in0=pt.t[:], in1=bc_f(NB.t[:, tl, jd:jd + 4]), op=ALU.mult), [pt, NB], [pt])
                S.op("pool", lambda e: e.tensor_tensor(out=Tacc.t[:], in0=pt.t[:], in1=bc_h(C.t[:, 0, :]), op=ALU.add), [pt, C], [Tacc])
                for h in range(4):
                    S.mm(b4.t[:, h * 128:(h + 1) * 128], [(pt.t[:, h, :], C.t[:, 0, :])], [pt, C], [b4])
                S.op("act", lambda e: e.activation(out=pp.t[:].rearrange("p h n -> p (h n)"), in_=b4.t[:, :], func=AF.Copy), [b4], [pp])
                for k in range(1, 6):
                    pt0, pp0 = Ptm[(k - 1) % 2], Pm[(k - 1) % 2]
                    pt1, pp1 = Ptm[k % 2], Pm[k % 2]
                    for h in range(4):
                        S.mm(b4.t[:, h * 128:(h + 1) * 128], [(pt0.t[:, h, :], pp0.t[:, h, :])], [pt0, pp0], [b4])
                    if k < 5:
                        for h in range(4):
                            S.mm(b5.t[:, h * 128:(h + 1) * 128], [(pp0.t[:, h, :], pt0.t[:, h, :])], [pt0, pp0], [b5])
                    S.op("act", lambda e: e.activation(out=pp1.t[:].rearrange("p h n -> p (h n)"), in_=b4.t[:, :], func=AF.Copy), [b4], [pp1])
                    if k < 5:
                        S.op("dve", lambda e: e.tensor_copy(out=pt1.t[:].rearrange("p h n -> p (h n)"), in_=b5.t[:, :]), [b5], [pt1])
                    for h in range(4):
                        S.mm(b6.t[:, h * 128:(h + 1) * 128], [(pp1.t[:, h, :], Tacc.t[:, h, :])], [pp1, Tacc], [b6])
                    S.op("dve", lambda e: e.tensor_tensor(out=Tacc.t[:], in0=v3(b6), in1=Tacc.t[:], op=ALU.add), [b6, Tacc], [Tacc])
                for h in range(4):
                    S.mm(b5.t[:, h * 128:(h + 1) * 128], [(vb.t[:, h, off:off + 128], self.cm(0, True))], [vb, self.Cb], [b5])
                S.op("act", lambda e: e.activation(out=Vtok.t[:].rearrange("p h n -> p (h n)"), in_=b5.t[:, :], func=AF.Copy), [b5], [Vtok])
                for h in range(4):
                    S.mm(b4.t[:, h * 128:(h + 1) * 128], [(kb_.t[:, h, off:off + 128], self.cm(0, True))], [kb_, self.Cb], [b4])
                S.op("dve", lambda e: e.tensor_tensor(out=Kg.t[:], in0=v3(b4), in1=bc_f(EG.t[:, 0:4]), op=ALU.mult), [b4, EG], [Kg])
                S.op("dve", lambda e: e.tensor_tensor(out=Ktail.t[:], in0=v3(b4), in1=bc_f(EG.t[:, 4:8]), op=ALU.mult), [b4, EG], [Ktail])
                for h in range(4):
                    S.mm(b1.t[:, h * 128:(h + 1) * 128], [(Tacc.t[:, h, :], Vtok.t[:, h, :])], [Tacc, Vtok], [b1])
                S.op("dve", lambda e: e.tensor_tensor(out=ub.t[:], in0=v3(b1), in1=bc_f(Bt.t[:, tl, jd:jd + 4]), op=ALU.mult), [b1, Bt], [ub])
                for h in range(4):
                    S.mm(b6.t[:, h * 128:(h + 1) * 128], [(Kg.t[:, h, :], Tacc.t[:, h, :])], [Kg, Tacc], [b6])
                S.op("act", lambda e: e.activation(out=wT.t[:].rearrange("p h n -> p (h n)"), in_=b6.t[:, :], func=AF.Copy), [b6], [wT])
                S.op("dve", lambda e: e.scalar_tensor_tensor(out=qk.t[:].rearrange("p h n -> p (h n)"), in0=b2.t[:, :], scalar=SC,
                                                              in1=decI.t[:].rearrange("p h n -> p (h n)"), op0=ALU.mult, op1=ALU.mult), [b2, decI], [qk])
                for h in range(4):
                    S.op("dve", lambda e, h=h: e.scalar_tensor_tensor(out=qd.t[:, h, :], in0=qb.t[:, h, off:off + 128], scalar=SC, in1=egb.t[:, h, :],
                                                                      op0=ALU.mult, op1=ALU.mult), [qb, egb], [qd])

            def scan_step(d, tl, ch):
                jd = 4 * d
                r0 = 64 * ch
                rows = slice(r0, r0 + 64)
                for h in range(4):
                    S.mm(b0.t[rows, h * 128:(h + 1) * 128], [(wT.t[:, h, rows], Sst.t[:, h, :])], [wT, Sst], [b0])
                S.op("dve", lambda e: e.tensor_tensor(out=vnew.t[rows], in0=b0.t[rows, :].rearrange("p (h n) -> p h n", h=4),
                                                      in1=bc_f(NB.t[rows, tl, jd:jd + 4], 64), op=ALU.mult), [b0, NB], [vnew])
                S.op("pool", lambda e: e.tensor_tensor(out=vnew.t[rows], in0=vnew.t[rows], in1=ub.t[rows], op=ALU.add), [vnew, ub], [vnew])
                for h in range(4):
                    oc_ = slice(h * 128 + r0, h * 128 + r0 + 64)
                    S.mm(b7.t[:, oc_], [(Sst.t[:, h, :], qd.t[:, h, rows]), (vnew.t[rows, h, :], qk.t[rows, h, rows])], [Sst, qd, vnew, qk], [b7])
                for h in range(4):
                    S.mm(b3.t[:, h * 128:(h + 1) * 128], [(Ktail.t[rows, h, :], vnew.t[rows, h, :])], [Ktail, vnew], [b3])
                S.op("dve", lambda e: e.tensor_tensor(out=Sst.t[:], in0=Sst.t[:], in1=bc_f(EGL.t[:, ch, :]), op=ALU.mult), [Sst, EGL], [Sst])
                S.op("dve", lambda e: e.tensor_tensor(out=Sst.t[:], in0=v3(b3), in1=Sst.t[:], op=ALU.add), [b3, Sst], [Sst])

            for d in range(2):
                S.op("pool", lambda e: e.memset(Sst.t[:], 0.0), [], [Sst])
                blocks = list(range(len(TILES)))
                if d == 1:
                    blocks = [0] + blocks[:0:-1]
                for bi, blk in enumerate(blocks):
                    t0, n_ = TILES[blk]
                    qb, kb_, vb = qB[bi % 2], kB[bi % 2], vB[bi % 2]
                    for h in range(4):
                        S.dma("sp", qb.t[:, h, 0:n_], self.dnq.t[h, :, t0:t0 + n_], [self.dnq], [qb])
                        S.dma("sp", kb_.t[:, h, 0:n_], self.dnk.t[h, :, t0:t0 + n_], [self.dnk], [kb_])
                        S.dma("sp", vb.t[:, h, 0:n_], self.dnv.t[h, :, t0:t0 + n_], [self.dnv], [vb])
                    subs = list(range(n_ // 128))
                    if d == 1:
                        subs = subs[::-1]
                    for si in subs:
                        tk = t0 + si * 128
                        tl = tk // 128
                        tile_local(d, tl, qb, kb_, vb, si * 128)
                        for ch in ((0, 1) if d == 0 else (1, 0)):
                            scan_step(d, tl, ch)
                        ofv = self.of.t[:, :, tk:tk + 128].rearrange("h p n -> p h n")
                        if d == 0:
                            S.op("act", lambda e: e.activation(out=ofst.t[:].rearrange("p h n -> p (h n)"), in_=b7.t[:, :], func=AF.Copy), [b7], [ofst])
                            S.dma("pool", ofv, ofst.t[:], [ofst], [self.of])
                        else:
                            S.dma("sp", ofl.t[:], ofv, [self.of], [ofl])
                            S.dma("sp", zt.t[:], self.dnz.t[:, :, tk:tk + 128].rearrange("h p n -> p h n"), [self.dnz], [zt])
                            S.op("dve", lambda e: e.tensor_tensor(out=osum.t[:], in0=v3(b7), in1=ofl.t[:], op=ALU.add), [b7, ofl], [osum])
                            S.op("act", lambda e: e.activation(out=sqn.t[:], in_=osum.t[:], func=AF.Square), [osum], [sqn])
                            S.mm(b2.t[:, :], [(self.cm(1, True), sqn.t[:].rearrange("p h n -> p (h n)"))], [self.Cb, sqn], [b2])
                            S.op("act", lambda e: e.activation(out=rsn.t[:].rearrange("p h n -> p (h n)"), in_=b2.t[:, :], func=AF.Sqrt, scale=1.0 / 128, bias=EPS), [b2], [rsn])
                            S.op("dve", lambda e: e.reciprocal(out=rsn.t[:], in_=rsn.t[:]), [rsn], [rsn])
                            S.op("dve", lambda e: e.scalar_tensor_tensor(out=onrm.t[:].rearrange("p h n -> p (h n)"), in0=osum.t[:].rearrange("p h n -> p (h n)"),
                                                                          scalar=sm["dnng"].t[:, l:l + 1], in1=rsn.t[:].rearrange("p h n -> p (h n)"),
                                                                          op0=ALU.mult, op1=ALU.mult), [osum, rsn, sm["dnng"]], [onrm])
                            S.op("pool", lambda e: e.tensor_tensor(out=ocst.t[:], in0=onrm.t[:], in1=zt.t[:], op=ALU.mult), [onrm, zt], [ocst])
                            S.dma("pool", self.oc.t[:, :, tk:tk + 128].rearrange("h p n -> p h n"), ocst.t[:], [ocst], [self.oc])
            S.barrier()


    def phase_merge(self, l):
        nc, S = self.nc, self.S
        last = (l == L - 1)
        with ExitStack() as es:
            wst = self.sb(es, "mw", [128, 8, 512], F32)
            Wbr = [self.sb(es, "mWbr%d" % r, [128, 4, 1024], BF16) for r in range(3)]
            Wo = self.sb(es, "mWo", [128, 8, 1024], BF16)
            for r in range(3):
                for hf in range(2):
                    S.dma("sp", wst.t[:, 0:4, :], self.w_br[r].t[l, :, hf * 512:(hf + 1) * 512].rearrange("(kc p) n -> p kc n", p=128), [], [wst])
                    S.op("pool", lambda e: e.tensor_copy(out=Wbr[r].t[:, :, hf * 512:(hf + 1) * 512], in_=wst.t[:, 0:4, :]), [wst], [Wbr[r]])
            for hf in range(2):
                S.dma("sp", wst.t[:], self.w_o.t[l, :, hf * 512:(hf + 1) * 512].rearrange("(kc p) n -> p kc n", p=128), [], [wst])
                S.op("pool", lambda e: e.tensor_copy(out=Wo.t[:, :, hf * 512:(hf + 1) * 512], in_=wst.t[:]), [wst], [Wo])
            ob_ = [[self.sb(es, "mo%d_%d" % (r, i), [128, 4, 512], BF16) for r in range(3)] for i in range(2)]
            gt = [self.sb(es, "mg%d" % i, [128, 24, 512], BF16) for i in range(2)]
            xt = [self.sb(es, "mx%d" % i, [128, 8, 512], F32) for i in range(2)]
            mrg = [self.sb(es, "mm%d" % i, [128, 8, 512], BF16) for i in range(2)]
            m1 = [self.sb(es, "m1_%d" % i, [128, 512], F32) for i in range(2)]
            m2 = [self.sb(es, "m2_%d" % i, [128, 512], F32) for i in range(2)]
            m3 = [self.sb(es, "m3_%d" % i, [128, 512], F32) for i in range(2)]
            srcs = (self.oa, self.ob, self.oc)
            xsv = self.xs.t.rearrange("(kc p) t -> p kc t", p=128)
            u = 0
            for ti, (t0, n_) in enumerate(TILES):
                if ti == 0 and last:
                    continue
                s = 1 if ti == 0 else 0
                o3, g, x, mg = ob_[ti % 2], gt[ti % 2], xt[ti % 2], mrg[ti % 2]
                for r in range(3):
                    S.dma("sp", o3[r].t[:, :, 0:n_], srcs[r].t[:, :, t0:t0 + n_].rearrange("h p n -> p h n"), [srcs[r]], [o3[r]])
                S.dma("sp", g.t[:, :, 0:n_], self.gat.t[:, :, t0:t0 + n_].rearrange("h p n -> p h n"), [self.gat], [g])
                S.dma("sp", x.t[:, :, 0:n_], xsv[:, :, t0:t0 + n_], [self.xs], [x])
                for j in range(8):
                    a1, a2, a3 = m1[j % 2], m2[j % 2], m3[j % 2]
                    pss = [self.psum[(3 * u + r) % 6] for r in range(3)]
                    u += 1
                    for r in range(3):
                        S.mm(pss[r].t[:, 0:n_], [(Wbr[r].t[:, kc, j * 128:(j + 1) * 128], o3[r].t[:, kc, 0:n_]) for kc in range(4)], [Wbr[r], o3[r]], [pss[r]])
                    for r, a in enumerate((a1, a2, a3)):
                        S.op("dve", lambda e, r=r, a=a: e.tensor_tensor(out=a.t[:, 0:n_], in0=pss[r].t[:, 0:n_], in1=g.t[:, 8 * r + j, 0:n_], op=ALU.mult), [pss[r], g], [a])
                    S.op("pool", lambda e: e.tensor_tensor(out=a1.t[:, 0:n_], in0=a1.t[:, 0:n_], in1=a2.t[:, 0:n_], op=ALU.add), [a1, a2], [a1])
                    S.op("pool", lambda e: e.tensor_tensor(out=mg.t[:, j, 0:n_], in0=a1.t[:, 0:n_], in1=a3.t[:, 0:n_], op=ALU.add), [a1, a3], [mg])
                for j in range(8):
                    ps = self.psum[6 + j % 2]
                    S.mm(ps.t[:, 0:n_], [(Wo.t[:, kc, j * 128:(j + 1) * 128], mg.t[:, kc, 0:n_]) for kc in range(8)], [Wo, mg], [ps])
                    S.op("dve", lambda e: e.scalar_tensor_tensor(out=x.t[:, j, 0:n_], in0=ps.t[:, 0:n_], scalar=self.MOD.t[:, l, 16 + j, s:s + 1], in1=x.t[:, j, 0:n_],
                                                                  op0=ALU.mult, op1=ALU.add), [ps, x, self.MOD], [x])
                S.dma("pool", xsv[:, :, t0:t0 + n_], x.t[:, :, 0:n_], [x], [self.xs])
            S.barrier()

    def phase_ffn1(self, l):
        nc, S = self.nc, self.S
        sm = self.small
        W = T + 4
        with ExitStack() as es:
            wst = [self.sb(es, "fw%d" % i, [128, 8, 256], F32) for i in range(2)]
            wbf = [self.sb(es, "fwb%d" % i, [128, 8, 256], BF16) for i in range(2)]
            RB = [self.sb(es, "fRB%d" % i, [128, W], F32) for i in range(2)]
            cvb = [self.sb(es, "fcv%d" % i, [128, W], F32) for i in range(2)]
            stg = [self.sb(es, "fst%d" % i, [128, 512], BF16) for i in range(3)]
            for rb in RB:
                S.op("pool", lambda e, rb=rb: e.memset(rb.t[:], 0.0), [], [rb])
            pieces = [(0, 0, 256)] + [(258 + 512 * i, 256 + 512 * i, 512) for i in range(8)]
            u = 0
            sidx = 0
            cw = sm["fcwT"].t
            n2 = W - 2
            def ldw(c):
                ws, wb = wst[c % 2], wbf[c % 2]
                S.dma("sp", ws.t[:, :, 0:128], self.ffn_w1.t[l, :, c * 128:(c + 1) * 128].rearrange("(kc p) n -> p kc n", p=128), [], [ws])
                S.dma("sp", ws.t[:, :, 128:256], self.ffn_w1.t[l, :, DFF + c * 128:DFF + (c + 1) * 128].rearrange("(kc p) n -> p kc n", p=128), [], [ws])
                S.op("pool", lambda e: e.tensor_copy(out=wb.t[:], in_=ws.t[:]), [ws], [wb])
            ldw(0)
            for c in range(22):
                rb, cv = RB[c % 2], cvb[c % 2]
                ws, wb = wst[c % 2], wbf[c % 2]
                if c + 1 < 22:
                    ldw(c + 1)
                for ti, (t0, n_) in enumerate(TILES):
                    pq = self.psum[u % 2]
                    u += 1
                    S.mm(pq.t[:, 0:n_], [(wb.t[:, kc, 0:128], self.hT.t[:, kc, t0:t0 + n_]) for kc in range(8)], [wb, self.hT], [pq])
                    pos = 1 + t0 if ti == 0 else 3 + t0
                    S.op("act", lambda e, pq=pq, pos=pos, n_=n_: e.activation(out=rb.t[:, pos:pos + n_], in_=pq.t[:, 0:n_], func=AF.Copy), [pq], [rb])
                S.op("dve", lambda e: e.tensor_scalar(out=cv.t[:, 0:n2], in0=rb.t[:, 0:n2], scalar1=cw[:, l, c, 0:1], scalar2=None, op0=ALU.mult), [rb, sm["fcwT"]], [cv])
                S.op("dve", lambda e: e.scalar_tensor_tensor(out=cv.t[:, 0:n2], in0=rb.t[:, 1:n2 + 1], scalar=cw[:, l, c, 1:2], in1=cv.t[:, 0:n2], op0=ALU.mult, op1=ALU.add), [rb, cv, sm["fcwT"]], [cv])
                S.op("dve", lambda e: e.scalar_tensor_tensor(out=cv.t[:, 0:n2], in0=rb.t[:, 2:n2 + 2], scalar=cw[:, l, c, 2:3], in1=cv.t[:, 0:n2], op0=ALU.mult, op1=ALU.add), [rb, cv, sm["fcwT"]], [cv])
                S.op("act", lambda e: e.activation(out=cv.t[:, 0:n2], in_=cv.t[:, 0:n2], func=AF.Silu, bias=sm["fcbT"].t[:, l, c:c + 1]), [cv, sm["fcbT"]], [cv])
                for ti, (t0, n_) in enumerate(TILES):
                    ci = pieces[ti][0]
                    pq = self.psum[2 + u % 2]
                    u += 1
                    so = stg[sidx % 3]
                    sidx += 1
                    S.mm(pq.t[:, 0:n_], [(wb.t[:, kc, 128:256], self.hT.t[:, kc, t0:t0 + n_]) for kc in range(8)], [wb, self.hT], [pq])
                    S.op("dve", lambda e, pq=pq, so=so, ci=ci, n_=n_: e.tensor_tensor(out=so.t[:, 0:n_], in0=pq.t[:, 0:n_], in1=cv.t[:, ci:ci + n_], op=ALU.mult), [pq, cv], [so])
                    S.dma("pool", self.gff.t[c, :, t0:t0 + n_], so.t[:, 0:n_], [so], [self.gff])
            S.barrier()

    def phase_ffn2(self, l):
        nc, S = self.nc, self.S
        last = (l == L - 1)
        with ExitStack() as es:
            wst = self.sb(es, "f2w", [128, 22, 256], F32)
            W2 = self.sb(es, "fW2", [128, 22, 1024], BF16)
            for q4 in range(4):
                S.dma("sp", wst.t[:], self.ffn_w2.t[l, :, q4 * 256:(q4 + 1) * 256].rearrange("(kc p) n -> p kc n", p=128), [], [wst])
                S.op("pool", lambda e: e.tensor_copy(out=W2.t[:, :, q4 * 256:(q4 + 1) * 256], in_=wst.t[:]), [wst], [W2])
            gt = [self.sb(es, "f2g%d" % i, [128, 22, 512], BF16) for i in range(2)]
            xt = [self.sb(es, "f2x%d" % i, [128, 8, 512], F32) for i in range(2)]
            xsv = self.xs.t.rearrange("(kc p) t -> p kc t", p=128)
            yv = self.yT.t.rearrange("(kc p) t -> p kc t", p=128)
            for ti, (t0, n_) in enumerate(TILES):
                if ti == 0 and last:
                    continue
                s = 1 if ti == 0 else 0
                g, x = gt[ti % 2], xt[ti % 2]
                S.dma("sp", g.t[:, :, 0:n_], self.gff.t[:, :, t0:t0 + n_].rearrange("h p n -> p h n"), [self.gff], [g])
                S.dma("sp", x.t[:, :, 0:n_], xsv[:, :, t0:t0 + n_], [self.xs], [x])
                for j in range(8):
                    ps = self.psum[j % 4]
                    S.mm(ps.t[:, 0:n_], [(W2.t[:, c, j * 128:(j + 1) * 128], g.t[:, c, 0:n_]) for c in range(22)], [W2, g], [ps])
                    S.op("dve", lambda e: e.scalar_tensor_tensor(out=x.t[:, j, 0:n_], in0=ps.t[:, 0:n_], scalar=self.MOD.t[:, l, 40 + j, s:s + 1], in1=x.t[:, j, 0:n_],
                                                                  op0=ALU.mult, op1=ALU.add), [ps, x, self.MOD], [x])
                if last:
                    S.dma("pool", yv[:, :, t0 - NCTX:t0 - NCTX + n_], x.t[:, :, 0:n_], [x], [self.yT])
                else:
                    S.dma("pool", xsv[:, :, t0:t0 + n_], x.t[:, :, 0:n_], [x], [self.xs])
            S.barrier()


def host_consts():
    c = np.zeros((128, 16, 128), np.float32)
    idx = np.arange(128)
    c[:, 0, :] = np.eye(128, dtype=np.float32)
    c[:, 1, :] = 1.0
    blk = (idx[:, None] // 64) == (idx[None, :] // 64)
    c[:, 2, :] = blk
    R = np.zeros((128, 128), np.float32)
    for m in range(128):
        if m % 32 < 16:
            R[m + 16, m] = -1.0
        else:
            R[m - 16, m] = 1.0
    c[:, 3, :] = R
    t_, c_ = idx[:, None], idx[None, :]
    c[:, 4, :] = blk & (t_ <= c_)
    c[:, 5, :] = blk & (t_ >= c_)
    c[:, 6, :] = blk & (t_ > c_)
    c[:, 7, :] = blk & (t_ < c_)
    c[:, 8, :] = -1.0
    c[:, 9, :] = np.where(blk & (c_ >= t_), 0.0, -30000.0)
    c[:, 10, :] = np.where(blk & (c_ <= t_), 0.0, -30000.0)
    c[:, 11, :] = blk & (c_ > t_)
    c[:, 12, :] = blk & (c_ < t_)
    return c


def rope_tables():
    rows = NLAT // 64
    row = np.repeat(np.arange(rows, dtype=np.int32), 64).astype(np.float32)
    col = np.tile(np.arange(64, dtype=np.int32), rows).astype(np.float32)
    d_axis = 32
    inv_freq = (np.float32(10000.0) ** (-np.arange(0, d_axis, 2, dtype=np.float32) / np.float32(d_axis))).astype(np.float32)
    ang_r = (row[:, None] * inv_freq[None, :]).astype(np.float32)
    ang_c = (col[:, None] * inv_freq[None, :]).astype(np.float32)
    cosT = np.ones((128, T), np.float32)
    sinT = np.zeros((128, T), np.float32)
    for p in range(128):
        d = p % 64
        ang = ang_r if d < 32 else ang_c
        f = d % 16
        cosT[p, NCTX:] = np.cos(ang[:, f])
        sinT[p, NCTX:] = np.sin(ang[:, f])
    return cosT, sinT


def pvec(v, chunks):
    return np.ascontiguousarray(np.asarray(v, np.float32).reshape(chunks, 128).T)


def make_in_maps(inp, nb=4):
    f = lambda a: np.ascontiguousarray(np.asarray(a, dtype=np.float32))
    cosT, sinT = rope_tables()
    shared = {
        "w_mod": f(inp["w_mod"]), "w_in": f(inp["w_in"]),
        "bmodT": np.ascontiguousarray(np.stack([pvec(inp["b_mod"][l], 48) for l in range(L)], axis=1)),
        "n1g": np.ascontiguousarray(np.stack([pvec(inp["norm1_g"][l], 8) for l in range(L)], axis=1)),
        "n2g": np.ascontiguousarray(np.stack([pvec(inp["norm2_g"][l], 8) for l in range(L)], axis=1)),
        "qkg": np.ascontiguousarray(np.stack([np.stack([np.tile(f(inp[k][l]), 2) for k in ("da_qn_g", "da_kn_g", "gq_qn_g", "gq_kn_g")], axis=1)
                                              for l in range(L)], axis=1)),
        "ropec": cosT, "ropes": sinT,
        "lamT": np.ascontiguousarray(np.transpose(f(inp["da_lambda"]), (2, 0, 1))),
        "sublnT": np.ascontiguousarray(f(inp["da_subln_g"]).T),
        "dnconvT": np.ascontiguousarray(np.transpose(f(inp["dn_conv_w"]).reshape(L, 3, 12, 128), (3, 0, 2, 1))),
        "dnalog": np.ascontiguousarray(np.broadcast_to(f(inp["dn_a_log"]).reshape(1, L, 8), (128, L, 8))),
        "dndtb": np.ascontiguousarray(np.broadcast_to(f(inp["dn_dt_bias"]).reshape(1, L, 8), (128, L, 8))),
        "dnng": np.ascontiguousarray(f(inp["dn_norm_g"]).T),
        "w_br_a": f(inp["w_br_a"]), "w_br_b": f(inp["w_br_b"]), "w_br_c": f(inp["w_br_c"]),
        "w_o": f(inp["w_o"]), "ffn_w1": f(inp["ffn_w1"]),
        "fcwT": np.ascontiguousarray(np.transpose(f(inp["ffn_conv_w"]).reshape(L, 3, 22, 128), (3, 0, 2, 1))),
        "fcbT": np.ascontiguousarray(np.transpose(f(inp["ffn_conv_b"]).reshape(L, 22, 128), (2, 0, 1))),
        "ffn_w2": f(inp["ffn_w2"]),
        "cst": host_consts(),
    }
    maps = []
    x, ctx, c, c_ctx = f(inp["x"]), f(inp["ctx"]), f(inp["c"]), f(inp["c_ctx"])
    for b in range(nb):
        m = dict(shared)
        m["xT"] = np.ascontiguousarray(np.concatenate([ctx[b], x[b]], axis=0).T)
        m["cT"] = np.ascontiguousarray(np.stack([pvec(c[b], 8), pvec(c_ctx, 8)], axis=2))
        maps.append(m)
    return maps


_PROG = {}


def kernel(**inp):
    if "p" not in _PROG:
        p = Prog()
        p.build()
        _PROG["p"] = p
    p = _PROG["p"]
    maps = make_in_maps(inp, 4)
    res = run_bass_kernel_spmd(p.nc, maps, core_ids=list(range(4)))
    out = np.stack([np.ascontiguousarray(res.results[b]["yT"].T) for b in range(4)], axis=0)
    return out.astype(np.float32)
```
